# Optimizing a Trainium2 kernel written in Bass

```python
import jax, jax.numpy as jnp
from jax import lax
import numpy as np

D_MODEL = 4096
BATCH = 4
SEQ = 2048
DEPTH = 2
DEC_BATCH = 32
DEC_SEQ = 8
PAST_LEN = 16384
PAGE_SIZE = 128

MIX_WIDTH = D_MODEL
ATT_WIDTH = MIX_WIDTH // 2
GM_WIDTH = MIX_WIDTH - ATT_WIDTH
HEAD_DIM = 64
N_HEADS = ATT_WIDTH // HEAD_DIM
N_KV_HEADS = 4
GQA_GROUP = N_HEADS // N_KV_HEADS
WINDOW = 128
CHUNK = 128
GM_HEAD_DIM = 128
GM_HEADS = GM_WIDTH // GM_HEAD_DIM
D_FF = ((8 * D_MODEL // 3 + 255) // 256) * 256
CONV_W = 3
EPS = 1e-6
Q_COLS = N_HEADS * HEAD_DIM
KV_COLS = N_KV_HEADS * HEAD_DIM
IN_COLS = Q_COLS + 2 * KV_COLS + 2 * GM_WIDTH

kernel_name = "hymba_style_swa_sink_gmlp_convffn_step"


def rmsnorm(x, g):
    xf = x.astype(jnp.float32)
    y = xf * lax.rsqrt(jnp.mean(xf * xf, axis=-1, keepdims=True) + EPS)
    return (y * g.astype(jnp.float32)).astype(x.dtype)


def sink_attention(q, k, v, q_pos, k_pos, sinks):
    B, N, Tq, H, D = q.shape
    qg = q.reshape(B, N, Tq, N_KV_HEADS, GQA_GROUP, D)
    s = jnp.einsum("bnqkgd,bnskd->bnkgqs", qg, k,
                   preferred_element_type=jnp.float32) * (D ** -0.5)
    rel = q_pos[:, :, None] - k_pos[:, None, :]
    mask = (rel >= 0) & (rel < WINDOW) & (k_pos[:, None, :] >= 0)
    s = jnp.where(mask[None, :, None, None], s, -jnp.inf)
    sink = sinks.astype(jnp.float32).reshape(N_KV_HEADS, GQA_GROUP)[None, None, :, :, None, None]
    full = jnp.concatenate([s, jnp.broadcast_to(sink, s.shape[:-1] + (1,))], axis=-1)
    p = jax.nn.softmax(full, axis=-1)[..., :-1]
    o = jnp.einsum("bnkgqs,bnskd->bnqkgd", p.astype(v.dtype), v)
    return o.reshape(B, N, Tq, H * D)


def swa_prompt(q, k, v, sinks):
    B, S = q.shape[:2]
    N = S // WINDOW
    qb = q.reshape(B, N, WINDOW, N_HEADS, HEAD_DIM)
    kb = k.reshape(B, N, WINDOW, N_KV_HEADS, HEAD_DIM)
    vb = v.reshape(B, N, WINDOW, N_KV_HEADS, HEAD_DIM)
    pad = ((0, 0), (1, 0), (0, 0), (0, 0), (0, 0))
    kk = jnp.concatenate([jnp.pad(kb, pad)[:, :-1], kb], axis=2)
    vv = jnp.concatenate([jnp.pad(vb, pad)[:, :-1], vb], axis=2)
    start = jnp.arange(N)[:, None] * WINDOW
    q_pos = start + jnp.arange(WINDOW)[None]
    k_pos = start - WINDOW + jnp.arange(2 * WINDOW)[None]
    o = sink_attention(qb, kk, vv, q_pos, k_pos, sinks)
    return o.reshape(B, S, ATT_WIDTH)


def swa_sample(q, k, v, k_buf, v_buf, sinks):
    B, T = q.shape[:2]
    L = k_buf.shape[1]
    kk = jnp.concatenate([k_buf, k], axis=1)[:, None]
    vv = jnp.concatenate([v_buf, v], axis=1)[:, None]
    q_pos = (PAST_LEN + jnp.arange(T))[None]
    k_pos = (PAST_LEN - L + jnp.arange(L + T))[None]
    o = sink_attention(q[:, None], kk, vv, q_pos, k_pos, sinks)
    return o.reshape(B, T, ATT_WIDTH)


def chunk_sgu(u, v, w_s, b_s):
    B, T = u.shape[:2]
    n = -(-T // CHUNK)
    vp = jnp.pad(v, ((0, 0), (0, n * CHUNK - T), (0, 0), (0, 0)))
    vp = vp.reshape(B, n, CHUNK, GM_HEADS, GM_HEAD_DIM)
    causal = jnp.tril(jnp.ones((CHUNK, CHUNK), dtype=bool))
    w = jnp.where(causal[None], w_s, jnp.zeros_like(w_s))
    mixed = jnp.einsum("hts,bnshe->bnthe", w, vp) + b_s.T[None, None, :, :, None]
    mixed = mixed.reshape(B, n * CHUNK, GM_HEADS, GM_HEAD_DIM)[:, :T]
    return (u * mixed).reshape(B, T, GM_WIDTH)


def token_mixer(h, k_buf, v_buf, w_in, w_out, sinks, w_s, b_s, g_gm):
    B, T, _ = h.shape
    z = jnp.einsum("btd,dc->btc", h, w_in)
    q = z[..., :Q_COLS].reshape(B, T, N_HEADS, HEAD_DIM)
    k = z[..., Q_COLS:Q_COLS + KV_COLS].reshape(B, T, N_KV_HEADS, HEAD_DIM)
    v = z[..., Q_COLS + KV_COLS:Q_COLS + 2 * KV_COLS].reshape(B, T, N_KV_HEADS, HEAD_DIM)
    gm = jax.nn.gelu(z[..., Q_COLS + 2 * KV_COLS:], approximate=True)
    u = gm[..., :GM_WIDTH].reshape(B, T, GM_HEADS, GM_HEAD_DIM)
    gv = rmsnorm(gm[..., GM_WIDTH:], g_gm).reshape(B, T, GM_HEADS, GM_HEAD_DIM)
    if k_buf is None:
        att = swa_prompt(q, k, v, sinks)
        keep = min(WINDOW, T)
        new_k, new_v = k[:, T - keep:], v[:, T - keep:]
    else:
        att = swa_sample(q, k, v, k_buf, v_buf, sinks)
        new_k, new_v = k, v
    gmo = chunk_sgu(u, gv, w_s, b_s)
    y = jnp.einsum("btc,cd->btd", jnp.concatenate([att, gmo], axis=-1), w_out)
    return y, new_k, new_v, gv


def conv_ffn(h, conv_buf, w_gate, w_up, w_down, conv_w, conv_b):
    T = h.shape[1]
    g = jnp.einsum("btd,df->btf", h, w_gate)
    up = jnp.einsum("btd,df->btf", h, w_up)
    ext = jnp.concatenate([conv_buf.astype(g.dtype), g], axis=1)
    c = conv_b
    for j in range(CONV_W):
        c = c + conv_w[j] * ext[:, j:j + T]
    a = jax.nn.gelu(c, approximate=True) * up
    y = jnp.einsum("btf,fd->btd", a, w_down)
    return y, ext[:, ext.shape[1] - (CONV_W - 1):]


def decoder_layer(x, k_buf, v_buf, conv_buf, w_in, w_out, sinks, w_s, b_s, g_gm,
                  g_pre_mix, g_post_mix, g_pre_ffn, g_post_ffn,
                  w_gate, w_up, w_down, conv_w, conv_b):
    m, new_k, new_v, gv = token_mixer(rmsnorm(x, g_pre_mix), k_buf, v_buf,
                                      w_in, w_out, sinks, w_s, b_s, g_gm)
    x = x + rmsnorm(m, g_post_mix)
    f, new_conv = conv_ffn(rmsnorm(x, g_pre_ffn), conv_buf, w_gate, w_up, w_down, conv_w, conv_b)
    x = x + rmsnorm(f, g_post_ffn)
    return x, new_k, new_v, new_conv, gv


def setup_inputs(seed: int = 0) -> dict:
    key = jax.random.key(seed)
    ks = jax.random.split(key, 24)
    f32 = jnp.float32
    nrm = lambda k, shape, scale: (jax.random.normal(k, shape, f32) * scale)
    buf = min(WINDOW, PAST_LEN)
    return {
        "x_prompt": nrm(ks[0], (BATCH, SEQ, D_MODEL), 1.0),
        "x_sample": nrm(ks[1], (DEC_BATCH, DEC_SEQ, D_MODEL), 1.0),
        "state_swa_k": nrm(ks[2], (DEPTH, DEC_BATCH, buf, N_KV_HEADS, HEAD_DIM), 1.0),
        "state_swa_v": nrm(ks[3], (DEPTH, DEC_BATCH, buf, N_KV_HEADS, HEAD_DIM), 1.0),
        "state_conv": nrm(ks[4], (DEPTH, DEC_BATCH, CONV_W - 1, D_FF), 1.0),
        "w_in": nrm(ks[5], (DEPTH, D_MODEL, IN_COLS), D_MODEL ** -0.5),
        "w_out": nrm(ks[6], (DEPTH, MIX_WIDTH, D_MODEL), MIX_WIDTH ** -0.5),
        "attn_sinks": nrm(ks[7], (DEPTH, N_HEADS), 0.5),
        "gm_spatial": nrm(ks[8], (DEPTH, GM_HEADS, CHUNK, CHUNK), CHUNK ** -0.5),
        "gm_bias": 1.0 + nrm(ks[9], (DEPTH, GM_HEADS, CHUNK), 0.1),
        "gm_norm": 1.0 + nrm(ks[10], (DEPTH, GM_WIDTH), 0.05),
        "norm_pre_mix": 1.0 + nrm(ks[11], (DEPTH, D_MODEL), 0.05),
        "norm_post_mix": 1.0 + nrm(ks[12], (DEPTH, D_MODEL), 0.05),
        "norm_pre_ffn": 1.0 + nrm(ks[13], (DEPTH, D_MODEL), 0.05),
        "norm_post_ffn": 1.0 + nrm(ks[14], (DEPTH, D_MODEL), 0.05),
        "w_ffn_gate": nrm(ks[15], (DEPTH, D_MODEL, D_FF), D_MODEL ** -0.5),
        "w_ffn_up": nrm(ks[16], (DEPTH, D_MODEL, D_FF), D_MODEL ** -0.5),
        "w_ffn_down": nrm(ks[17], (DEPTH, D_FF, D_MODEL), D_FF ** -0.5),
        "conv_w": nrm(ks[18], (DEPTH, CONV_W, D_FF), CONV_W ** -0.5),
        "conv_b": nrm(ks[19], (DEPTH, D_FF), 0.02),
    }


def reference(x_prompt, x_sample, state_swa_k, state_swa_v, state_conv,
              w_in, w_out, attn_sinks, gm_spatial, gm_bias, gm_norm,
              norm_pre_mix, norm_post_mix, norm_pre_ffn, norm_post_ffn,
              w_ffn_gate, w_ffn_up, w_ffn_down, conv_w, conv_b):
    xp, xs = x_prompt, x_sample
    p_k, p_v, p_c, s_k, s_v, s_c, s_g = [], [], [], [], [], [], []
    zero_conv = jnp.zeros((xp.shape[0], CONV_W - 1, D_FF), xp.dtype)
    for l in range(DEPTH):
        w = (w_in[l], w_out[l], attn_sinks[l], gm_spatial[l], gm_bias[l], gm_norm[l],
             norm_pre_mix[l], norm_post_mix[l], norm_pre_ffn[l], norm_post_ffn[l],
             w_ffn_gate[l], w_ffn_up[l], w_ffn_down[l], conv_w[l], conv_b[l])
        xp, nk, nv, nc, _ = decoder_layer(xp, None, None, zero_conv, *w)
        p_k.append(nk); p_v.append(nv); p_c.append(nc)
        xs, nk, nv, nc, gv = decoder_layer(xs, state_swa_k[l], state_swa_v[l], state_conv[l], *w)
        s_k.append(nk); s_v.append(nv); s_c.append(nc); s_g.append(gv)
    return (xp, xs,
            jnp.stack(p_k), jnp.stack(p_v), jnp.stack(p_c),
            jnp.stack(s_k), jnp.stack(s_v), jnp.stack(s_c), jnp.stack(s_g))
```

```python
import numpy as np
import concourse.bass as bass
import concourse.mybir as mybir
from concourse.bass_utils import run_bass_kernel_spmd

F32 = mybir.dt.float32
BF16 = mybir.dt.bfloat16
AF = mybir.ActivationFunctionType
ALU = mybir.AluOpType
AX = mybir.AxisListType

FULL_CFG = dict(D=4096, SEQ=2048, CSEQ=1280, OWN_A=1152, BATCH=4, DEC_BATCH=32, DEC_SEQ=8, DEPTH=2,
                PASSES=[3, 3, 2, 2], NCORES=8)
EPS = 1e-6
DEBUG = False
LAST_DBG = {}
ARENA_WORDS = 52000
WSLOTS = 3
WSLOT_ELEMS = 4096


def derive(cfg):
    c = dict(cfg)
    D = c["D"]
    c["ATT"] = D // 2
    c["GM"] = D - c["ATT"]
    c["HD"] = 64
    c["NH"] = c["ATT"] // 64
    c["NKV"] = 4
    c["GQA"] = c["NH"] // c["NKV"]
    c["CPK"] = c["GQA"] // 2
    c["GH"] = c["GM"] // 128
    c["F"] = ((8 * D // 3 + 255) // 256) * 256
    c["KC"] = D // 128
    c["FC"] = c["F"] // 128
    c["QC"] = c["ATT"] // 128
    c["UC"] = c["GM"] // 128
    c["KVW"] = c["NKV"] * 64
    c["IN"] = c["ATT"] + 2 * c["KVW"] + 2 * c["GM"]
    c["NS"] = c["DEC_BATCH"] // c["NCORES"]
    c["S"] = c["NS"] * c["DEC_SEQ"]
    c.setdefault("CSEQ", c["SEQ"])
    c.setdefault("OWN_A", c["CSEQ"])
    assert sum(c["PASSES"]) * 128 == c["CSEQ"]
    if c["CSEQ"] < c["SEQ"]:
        assert c["NCORES"] == 2 * c["BATCH"]
        assert c["SEQ"] - c["CSEQ"] + 3 * 128 <= c["OWN_A"] <= c["CSEQ"]
    assert c["KC"] % 8 == 0 and c["CPK"] >= 1 and c["DEC_SEQ"] == 8 and c["NS"] == 4
    return c


class Prog:
    ENGS = ("pe", "act", "dve", "pool", "sp")

    def __init__(self):
        self.ops = {e: [] for e in self.ENGS}
        self.res = {}
        self.dma_cnt = {}

    def op(self, eng, fn, R=(), W=(), key=None):
        lst = self.ops[eng]
        me = (eng, len(lst))
        deps = set()
        for r in R:
            st = self.res.get(r)
            if st and st[0] is not None:
                deps.add(st[0])
        for w in W:
            st = self.res.get(w)
            if st:
                if st[0] is not None and st[0][0] != eng:
                    deps.add(st[0])
                for rd in st[1]:
                    if rd[0] != eng:
                        deps.add(rd)
        deps.discard(me)
        rec = dict(fn=fn, deps=deps, key=key, val=None, needed=False)
        if key is not None:
            self.dma_cnt[key] = self.dma_cnt.get(key, 0) + 1
            rec["val"] = 16 * self.dma_cnt[key]
        lst.append(rec)
        for d in deps:
            self.ops[d[0]][d[1]]["needed"] = True
        for r in R:
            self.res.setdefault(r, [None, []])[1].append(me)
        for w in W:
            self.res[w] = [me, []]
        return me

    def emit(self, nc, block, sems, dsems):
        cum = {}
        for e in self.ENGS:
            c = 0
            arr = []
            for rec in self.ops[e]:
                if rec["needed"] and rec["key"] is None:
                    c += 1
                arr.append(c)
            cum[e] = arr

        def target(dep):
            rec = self.ops[dep[0]][dep[1]]
            if rec["key"] is not None:
                return dsems[rec["key"]], rec["val"]
            return sems[dep[0]], cum[dep[0]][dep[1]]

        def run(e, handle):
            waited = {}
            for rec in self.ops[e]:
                need = {}
                for d in rec["deps"]:
                    s, v = target(d)
                    k = id(s)
                    if waited.get(k, (None, 0))[1] >= v:
                        continue
                    if k not in need or need[k][1] < v:
                        need[k] = (s, v)
                for k, (s, v) in need.items():
                    handle.wait_ge(s, v)
                    waited[k] = (s, v)
                ins = rec["fn"](handle)
                if rec["key"] is not None:
                    ins.then_inc(dsems[rec["key"]], 16)
                elif rec["needed"]:
                    ins.then_inc(sems[e], 1)
            done = {}
            for rec in self.ops[e]:
                if rec["key"] is not None:
                    done[rec["key"]] = max(done.get(rec["key"], 0), rec["val"])
            for k, v in done.items():
                handle.wait_ge(dsems[k], v)

        @block.tensor
        def _(t):
            run("pe", t)

        @block.scalar
        def _(a):
            run("act", a)

        @block.vector
        def _(v):
            run("dve", v)

        @block.gpsimd
        def _(g):
            run("pool", g)

        @block.sync
        def _(s):
            run("sp", s)


def build_nc(cfg):
    c = derive(cfg)
    D, KC, FC, F, QC, UC, GH = c["D"], c["KC"], c["FC"], c["F"], c["QC"], c["UC"], c["GH"]
    NKV, CPK, KVW, IN, GM, ATT = c["NKV"], c["CPK"], c["KVW"], c["IN"], c["GM"], c["ATT"]
    L, SEQ, S, NS = c["DEPTH"], c["CSEQ"], c["S"], c["NS"]
    PASSES = c["PASSES"]
    NPASS = len(PASSES)
    QSCALE = 64 ** -0.5
    RND = min(16, c["KC"])

    nc = bass.Bass("TRN2", target_bir_lowering=False)

    def din(name, shape):
        return nc.dram_tensor(name, list(shape), F32, kind="ExternalInput").ap()

    def dout(name, shape):
        return nc.dram_tensor(name, list(shape), F32, kind="ExternalOutput").ap()

    xp = din("xp", [SEQ, D]); xs = din("xs", [S, D])
    sk = din("sk", [L, NS, 128, KVW]); sv = din("sv", [L, NS, 128, KVW])
    sc = din("sc", [L, NS, 2, F])
    w_in = din("w_in", [L, D, IN]); w_out = din("w_out", [L, D, D])
    w_gate = din("w_gate", [L, D, F]); w_up = din("w_up", [L, D, F]); w_down = din("w_down", [L, F, D])
    sinks = din("sinks", [L, c["NH"]]); gms = din("gms", [L, GH, 128, 128]); gmb = din("gmb", [L, GH, 128])
    gmn = din("gmn", [L, GM])
    g_pre_mix = din("g_pre_mix", [L, D]); g_post_mix = din("g_post_mix", [L, D])
    g_pre_ffn = din("g_pre_ffn", [L, D]); g_post_ffn = din("g_post_ffn", [L, D])
    cw = din("cw", [L, 3, F]); cb = din("cb", [L, F])
    c_ident = din("c_ident", [128, 128]); c_mcur = din("c_mcur", [128, 128]); c_mprev = din("c_mprev", [128, 128])
    c_mhist = din("c_mhist", [128, NS, S]); c_mnew = din("c_mnew", [S, S])

    yp = dout("yp", [SEQ, D]); ys = dout("ys", [S, D])
    pk = dout("pk", [L, 128, KVW]); pv = dout("pv", [L, 128, KVW]); pc = dout("pc", [L, 2, F])
    sko = dout("sko", [L, S, KVW]); svo = dout("svo", [L, S, KVW]); sco = dout("sco", [L, NS, 2, F])
    sgo = dout("sgo", [L, S, GM])

    P = Prog()
    ctx = {}
    dbg_list = []

    def dbg(name, ap, res, shape):
        if not DEBUG or name in [d[0] for d in dbg_list]:
            return
        t = nc.dram_tensor("dbg_" + name, list(shape), F32, kind="ExternalOutput").ap()
        dbg_list.append((name, shape))
        eng = "pool" if ap.dtype == BF16 else "sp"
        P.op(eng, lambda e: e.dma_start(out=t, in_=ap), R=[res], W=[], key="dbg_" + name)

    def body(arena, banks):
        def carve(off, shape, dt):
            n = int(np.prod(shape[1:]))
            if dt == BF16:
                assert n % 2 == 0
                words = n // 2
                v = arena[:, off:off + words].bitcast(BF16)
            else:
                words = n
                v = arena[:, off:off + words]
            if len(shape) == 3:
                v = v.rearrange("p (a b) -> p a b", b=shape[2])
            elif len(shape) == 4:
                v = v.rearrange("p (a b c) -> p a b c", b=shape[2], c=shape[3])
            return v, off + words

        class Bump:
            def __init__(self, base, limit):
                self.o = base; self.limit = limit

            def __call__(self, shape, dt):
                v, self.o = carve(self.o, shape, dt)
                assert self.o <= self.limit, ("arena overflow", self.o, self.limit)
                return v

        pers = Bump(0, ARENA_WORDS)
        ident = pers([128, 128], F32)
        ones_b = pers([128, 128], BF16)
        mcur = pers([128, 128], BF16); mprev = pers([128, 128], BF16)
        mhist = pers([128, NS, S], BF16); mnew = pers([128, S], BF16)
        gains = {nm: pers([128, L * KC], F32) for nm in ("pre_mix", "post_mix", "pre_ffn", "post_ffn")}
        cwt = pers([128, L * 3 * FC], F32)
        cbt = pers([128, L * FC], F32)
        convcar = [pers([128, FC, 2], F32) for _ in range(L)]
        kcar = [pers([128, NKV, 128], BF16) for _ in range(L)]
        vcar = [pers([128, KVW], BF16) for _ in range(L)]
        sinkexp = pers([128, L * QC], F32)
        rstd = pers([128, 512], F32); rtmp = pers([128, 512], F32)
        tst = [pers([128, 128], F32) for _ in range(2)]
        epsb = pers([128, 2], F32)
        wslots = [pers([128, WSLOT_ELEMS], BF16) for _ in range(WSLOTS)]
        fcell = pers([128, 2], F32)
        PERS_END = pers.o

        OVL = ["my", "xstage", "xo", "kdup", "kv32_0", "kv32_1", "gv32", "ss", "ggm_bc", "bbc", "wsT", "rt", "tmp",
               "skst", "skd", "wsamp32", "wsamp", "gext", "scs", "gtail", "ostg", "ptail", "ostg2"] + \
              ["vtm%d" % i for i in range(5)] + ["gv%d" % i for i in range(5)] + ["P%d" % i for i in range(6)] + \
              ["khd%d" % i for i in range(4)] + ["vh%d" % i for i in range(4)] + \
              ["cbuf%d" % i for i in range(4)] + ["gel%d" % i for i in range(4)]

        def fence():
            P.op("dve", lambda e: e.memset(fcell, 0.0), W=OVL)

        bank_rr = [0]

        def nbank():
            b = bank_rr[0] % 8
            bank_rr[0] += 1
            return b

        def Bk(i):
            return banks[i]

        def bres(i):
            return "bank%d" % i

        def dma(eng, out, in_, R, W, key, slow=False):
            if slow:
                P.op(eng, lambda e: e.dma_start(out=out, in_=in_, allow_slow_non_contiguous=True), R=R, W=W, key=key)
            else:
                P.op(eng, lambda e: e.dma_start(out=out, in_=in_), R=R, W=W, key=key)

        tcount = [0]

        def tload(dst, dstres, src_rows, R_):
            i = tcount[0] % 2
            tcount[0] += 1
            st = tst[i]
            dma("sp", st[0:R_, :], src_rows, R=[], W=["tst%d" % i], key="tst%d" % i)
            b = nbank()
            P.op("pe", lambda e: e.transpose(Bk(b)[:, 0:R_], st[0:R_, :], ident[0:R_, 0:R_]),
                 R=["tst%d" % i, "ident"], W=[bres(b)])
            P.op("act", lambda e: e.copy(out=dst, in_=Bk(b)[:, 0:R_]), R=[bres(b)], W=[dstres])

        dma("sp", ident, c_ident, R=[], W=["ident"], key="c0")
        dma("pool", mcur, c_mcur, R=[], W=["mcur"], key="c1")
        dma("pool", mprev, c_mprev, R=[], W=["mprev"], key="c2")
        dma("pool", mhist, c_mhist, R=[], W=["mhist"], key="c3")
        dma("pool", mnew[0:S, :], c_mnew, R=[], W=["mnew"], key="c4")
        P.op("dve", lambda e: e.memset(ones_b, 1.0), W=["ones"])
        P.op("dve", lambda e: e.memset(epsb, EPS), W=["eps"])
        for l in range(L):
            P.op("dve", lambda e, l=l: e.memset(convcar[l], 0.0), W=["convcar%d" % l])
            P.op("dve", lambda e, l=l: e.memset(kcar[l], 0.0), W=["kcar%d" % l])
            P.op("dve", lambda e, l=l: e.memset(vcar[l], 0.0), W=["vcar%d" % l])
        for nm, src in (("pre_mix", g_pre_mix), ("post_mix", g_post_mix), ("pre_ffn", g_pre_ffn), ("post_ffn", g_post_ffn)):
            rows = src.rearrange("l (k p) -> (l k) p", p=128)
            for r0 in range(0, L * KC, 128):
                n = min(128, L * KC - r0)
                tload(gains[nm][:, r0:r0 + n], "g_" + nm, rows[r0:r0 + n, :], n)
        rows = cw.rearrange("l j (f p) -> (l j f) p", p=128)
        for r0 in range(0, L * 3 * FC, 128):
            n = min(128, L * 3 * FC - r0)
            tload(cwt[:, r0:r0 + n], "cwt", rows[r0:r0 + n, :], n)
        rows = cb.rearrange("l (f p) -> (l f) p", p=128)
        for r0 in range(0, L * FC, 128):
            n = min(128, L * FC - r0)
            tload(cbt[:, r0:r0 + n], "cbt", rows[r0:r0 + n, :], n)
        for l in range(L):
            for par in range(2):
                src = sinks[l].rearrange("(c two) -> c two", two=2)[:, par].partition_broadcast(64)
                dma("sp", sinkexp[par * 64:(par + 1) * 64, l * QC:(l + 1) * QC], src, R=[], W=["sinkexp"],
                    key="c5", slow=True)
        P.op("act", lambda e: e.activation(out=sinkexp, in_=sinkexp, func=AF.Exp), R=["sinkexp"], W=["sinkexp"])

        def plan_tiles():
            tiles = []
            for _p in range(NPASS):
                for l in range(L):
                    wv = w_in[l].rearrange("(k p) c -> p k c", p=128)
                    cols = [(c0, 512) for c0 in range(0, ATT, 512)] + [(ATT, 2 * KVW)] + \
                           [(c0, 512) for c0 in range(ATT + 2 * KVW, IN, 512)]
                    for (c0, w) in cols:
                        for k0 in range(0, KC, 8):
                            tiles.append((wv[:, k0:k0 + 8, c0:c0 + w], 8, w))
                    wv = w_out[l].rearrange("(k p) c -> p k c", p=128)
                    for c0 in range(0, D, 512):
                        for k0 in range(0, KC, 8):
                            tiles.append((wv[:, k0:k0 + 8, c0:c0 + 512], 8, 512))
                    gv_ = w_gate[l].rearrange("(k p) c -> p k c", p=128)
                    uv_ = w_up[l].rearrange("(k p) c -> p k c", p=128)
                    dv_ = w_down[l].rearrange("(f p) c -> p f c", p=128)
                    for r0 in range(0, FC, RND):
                        r1 = min(FC, r0 + RND)
                        for f0 in range(r0, r1, 4):
                            n = min(4, r1 - f0)
                            for wvv in (gv_, uv_):
                                for k0 in range(0, KC, 8):
                                    tiles.append((wvv[:, k0:k0 + 8, f0 * 128:(f0 + n) * 128], 8, n * 128))
                        for c0 in range(0, D, 512):
                            for t0 in range(r0, r1, 8):
                                n = min(8, r1 - t0)
                                tiles.append((dv_[:, t0:t0 + n, c0:c0 + 512], n, 512))
            return tiles

        tiles = plan_tiles()
        wstate = dict(next_issue=0, next_use=0)

        def w_issue():
            i = wstate["next_issue"]
            if i >= len(tiles):
                return
            src, a, b = tiles[i]
            s = i % WSLOTS
            dst = wslots[s][:, 0:a * b].rearrange("p (a b) -> p a b", b=b)
            dma("pool", dst, src, R=[], W=["wslot%d" % s], key="w%d" % s)
            wstate["next_issue"] += 1

        def w_next(a, b):
            i = wstate["next_use"]
            while wstate["next_issue"] < min(len(tiles), i + WSLOTS):
                w_issue()
            src, ta, tb = tiles[i]
            assert (ta, tb) == (a, b), (i, ta, tb, a, b)
            s = i % WSLOTS
            wstate["next_use"] += 1
            return wslots[s][:, 0:a * b].rearrange("p (a b) -> p a b", b=b), "wslot%d" % s

        def do_pass(ps, tok0):
            NB = PASSES[ps]
            T = NB * 128
            has_s = (ps == NPASS - 1)
            last_pass = (ps == NPASS - 1)
            SS = S if has_s else 0
            TT = T + SS
            assert TT <= 512
            NBLK = NB + (1 if has_s else 0)
            TTX = 2 + T + (NS * 10 if has_s else 0)

            pb = Bump(PERS_END, ARENA_WORDS)
            x = pb([128, KC, TT], F32)
            h = pb([128, KC, TT], BF16)
            qu = pb([128, KC, TT], BF16)
            QU_BASE = pb.o - (KC * TT) // 2
            my_base = pb.o
            my = pb([128, KC, TT], F32)
            MY_END = pb.o

            def blk_rows(bi):
                return 128 if bi < NB else SS

            def blk_cols(bi):
                return (bi * 128, 128) if bi < NB else (T, SS)

            fence()
            stg = Bump(my_base, ARENA_WORDS)
            xstage = stg([128, D], F32)
            for bi in range(NBLK):
                n = blk_rows(bi); c0, _ = blk_cols(bi)
                src = xp[tok0 + bi * 128: tok0 + bi * 128 + 128, :] if bi < NB else xs[:, :]
                dma("sp", xstage[0:n, :], src, R=[], W=["xstage"], key="xin")
                for g0 in range(0, KC, 4):
                    b = nbank()
                    for i in range(4):
                        kc = g0 + i
                        P.op("pe", lambda e, b=b, i=i, kc=kc, n=n: e.transpose(
                            Bk(b)[:, i * 128:i * 128 + n], xstage[0:n, kc * 128:(kc + 1) * 128], ident[0:n, 0:n]),
                            R=["xstage", "ident"], W=[bres(b)])
                    src_v = Bk(b)[:, :].rearrange("p (a t) -> p a t", t=128)[:, :, 0:n]
                    P.op("act", lambda e, g0=g0, c0=c0, n=n, src_v=src_v: e.copy(out=x[:, g0:g0 + 4, c0:c0 + n], in_=src_v),
                         R=[bres(b)], W=["x"])

            dbg("xin_p%d" % ps, x, "x", [128, KC, TT])

            def stat_rstd(src, srcres, sq, sqres, nfeat_chunks, outres):
                for g0 in range(0, nfeat_chunks, 8):
                    P.op("act", lambda e, g0=g0: e.activation(out=sq[:, g0:g0 + 8, :], in_=src[:, g0:g0 + 8, :], func=AF.Square),
                         R=[srcres], W=[sqres])
                b = nbank()
                for kc in range(nfeat_chunks):
                    P.op("pe", lambda e, kc=kc, b=b: e.matmul(Bk(b)[:, 0:TT], lhsT=ones_b, rhs=sq[:, kc, :],
                                                              start=(kc == 0), stop=(kc == nfeat_chunks - 1)),
                         R=[sqres, "ones"], W=[bres(b)])
                P.op("act", lambda e, b=b: e.activation(out=rtmp[:, 0:TT], in_=Bk(b)[:, 0:TT], func=AF.Sqrt,
                                                        bias=epsb[:, 0:1], scale=1.0 / (nfeat_chunks * 128)),
                     R=[bres(b), "eps"], W=["rtmp"])
                P.op("dve", lambda e: e.reciprocal(out=rstd[:, 0:TT], in_=rtmp[:, 0:TT]), R=["rtmp"], W=[outres])

            def prenorm(l, gname):
                stat_rstd(x, "x", qu, "qu", KC, "rstd")
                g = gains[gname]
                for kc in range(KC):
                    P.op("dve", lambda e, kc=kc: e.scalar_tensor_tensor(
                        out=h[:, kc, :], in0=x[:, kc, :], scalar=g[:, l * KC + kc:l * KC + kc + 1], in1=rstd[:, 0:TT],
                        op0=ALU.mult, op1=ALU.mult), R=["x", "rstd", "g_" + gname], W=["h"])

            def postnorm(l, gname):
                stat_rstd(my, "my", h, "h", KC, "rstd")
                g = gains[gname]
                for kc in range(KC):
                    P.op("dve", lambda e, kc=kc: e.scalar_tensor_tensor(
                        out=my[:, kc, :], in0=my[:, kc, :], scalar=g[:, l * KC + kc:l * KC + kc + 1], in1=rstd[:, 0:TT],
                        op0=ALU.mult, op1=ALU.mult), R=["my", "rstd", "g_" + gname], W=["my"])
                for g0 in range(0, KC, 8):
                    P.op("dve", lambda e, g0=g0: e.tensor_tensor(out=x[:, g0:g0 + 8, :], in0=x[:, g0:g0 + 8, :],
                                                                 in1=my[:, g0:g0 + 8, :], op=ALU.add),
                         R=["x", "my"], W=["x"])

            def fm_panel(hsrc, hres, nchunks, evac, bankset):
                w = nchunks * 128
                for k0 in range(0, KC, 8):
                    wt, wres = w_next(8, w)
                    for kl in range(8):
                        kc = k0 + kl
                        for i in range(nchunks):
                            b = bankset[i]
                            P.op("pe", lambda e, wt=wt, kl=kl, i=i, kc=kc, b=b: e.matmul(
                                Bk(b)[:, 0:TT], lhsT=wt[:, kl, i * 128:(i + 1) * 128], rhs=hsrc[:, kc, :],
                                start=(kc == 0), stop=(kc == KC - 1)), R=[wres, hres], W=[bres(b)])
                for i in range(nchunks):
                    evac(i, bankset[i])

            def do_layer(l):
                prenorm(l, "pre_mix")
                dbg("h_p%d_l%d" % (ps, l), h, "h", [128, KC, TT])
                dbg("rstd_p%d_l%d" % (ps, l), rstd[:, 0:TT], "rstd", [128, TT])
                fence()

                mt = Bump(my_base, ARENA_WORDS)
                kdup = mt([128, NKV, TT], BF16)
                vtm = [mt([128, KVW], BF16) for _ in range(NBLK)]
                kv32 = [mt([128, 2 * KVW], F32) for _ in range(2)]
                gv = [mt([128, GM], BF16) for _ in range(NBLK)]
                ss = mt([128, 8 * NBLK], F32)
                ggm_bc = mt([128, GM], F32)
                bbc = mt([128, GH, 128], F32)
                wsT = mt([128, GH, 128], BF16)
                Pb = [mt([128, 512], BF16) for _ in range(6)]
                rt = mt([128, 512], F32)
                tmp = mt([128, 512], F32)
                if has_s:
                    gv32 = mt([128, GM], F32)
                    khd = [mt([128, NKV, 128], BF16) for _ in range(NS)]
                    vh = [mt([128, KVW], BF16) for _ in range(NS)]
                    skst = mt([128, KVW], F32)
                    skd = mt([128, NKV, 2, 64], F32)
                    wsamp32 = mt([128, GH, S], F32)
                    wsamp = mt([128, GH, S], BF16)

                dma("sp", ggm_bc, gmn[l].partition_broadcast(128),
                    R=[], W=["ggm_bc"], key="bc0")
                dma("sp", bbc, gmb[l].rearrange("h t -> (h t)").partition_broadcast(128).rearrange("p (h t) -> p h t", t=128),
                    R=[], W=["bbc"], key="bc1")
                for hd in range(GH):
                    i = tcount[0] % 2
                    tcount[0] += 1
                    st = tst[i]
                    dma("sp", st, gms[l, hd], R=[], W=["tst%d" % i], key="tst%d" % i)
                    b = nbank()
                    P.op("pe", lambda e, b=b, st=st: e.transpose(Bk(b)[:, 0:128], st, ident), R=["tst%d" % i, "ident"], W=[bres(b)])
                    P.op("dve", lambda e, b=b, hd=hd: e.tensor_tensor(out=wsT[:, hd, :], in0=Bk(b)[:, 0:128], in1=mcur, op=ALU.mult),
                         R=[bres(b), "mcur"], W=["wsT"])
                P.op("dve", lambda e: e.memset(ss, 0.0), W=["ss"])
                if has_s:
                    P.op("dve", lambda e: e.memset(wsamp, 0.0), W=["wsamp"])
                    for bq in range(NS):
                        dma("sp", wsamp[bq * 8:(bq + 1) * 8, :, bq * 8:(bq + 1) * 8], wsT[0:8, :, 0:8], R=["wsT"], W=["wsamp"],
                            key="wsamp")
                    for bq in range(NS):
                        dma("sp", skst, sk[l, bq], R=[], W=["skst"], key="skst")
                        P.op("dve", lambda e: e.tensor_copy(
                            out=skd, in_=skst.rearrange("p (j d) -> p j d", d=64).unsqueeze(2).to_broadcast([128, NKV, 2, 64])),
                            R=["skst"], W=["skd"])
                        for j in range(NKV):
                            b = nbank()
                            P.op("pe", lambda e, b=b, j=j: e.transpose(Bk(b)[:, 0:128], skd[:, j].rearrange("p a d -> p (a d)"), ident),
                                 R=["skd", "ident"], W=[bres(b)])
                            P.op("act", lambda e, b=b, j=j, bq=bq: e.copy(out=khd[bq][:, j, :], in_=Bk(b)[:, 0:128]),
                                 R=[bres(b)], W=["khd%d" % bq])
                        dma("pool", vh[bq], sv[l, bq], R=[], W=["vh%d" % bq], key="vh%d" % bq)

                pcount = [0]

                def bset():
                    s_ = [0, 1, 2, 3] if pcount[0] % 2 == 0 else [4, 5, 6, 7]
                    pcount[0] += 1
                    return s_

                for p0 in range(0, QC, 4):
                    def ev_q(i, b, p0=p0):
                        P.op("act", lambda e: e.mul(out=qu[:, p0 + i, :], in_=Bk(b)[:, 0:TT], mul=QSCALE),
                             R=[bres(b)], W=["qu"])
                    fm_panel(h, "h", 4, ev_q, bset())

                tb = list(range(NBLK))
                kb = [4, 5, 6, 7][:NKV]
                assert NBLK <= 4
                for k0 in range(0, KC, 8):
                    wt, wres = w_next(8, 2 * KVW)
                    for kl in range(8):
                        kc = k0 + kl
                        for bi in range(NBLK):
                            n = blk_rows(bi); c0, _ = blk_cols(bi)
                            P.op("pe", lambda e, wt=wt, kl=kl, kc=kc, bi=bi, n=n, c0=c0: e.matmul(
                                Bk(tb[bi])[0:n, 0:2 * KVW], lhsT=h[:, kc, c0:c0 + n], rhs=wt[:, kl, :],
                                start=(kc == 0), stop=(kc == KC - 1)), R=[wres, "h"], W=[bres(tb[bi])])
                        for j in range(NKV):
                            for par in range(2):
                                P.op("pe", lambda e, wt=wt, kl=kl, kc=kc, j=j, par=par: e.matmul(
                                    Bk(kb[j])[par * 64:(par + 1) * 64, 0:TT], lhsT=wt[:, kl, j * 64:(j + 1) * 64], rhs=h[:, kc, :],
                                    start=(kc == 0), stop=(kc == KC - 1)), R=[wres, "h"], W=[bres(kb[j])])
                for j in range(NKV):
                    P.op("act", lambda e, j=j: e.copy(out=kdup[:, j, :], in_=Bk(kb[j])[:, 0:TT]), R=[bres(kb[j])], W=["kdup"])
                for bi in range(NBLK):
                    n = blk_rows(bi)
                    P.op("act", lambda e, bi=bi, n=n: e.copy(out=vtm[bi][0:n, :], in_=Bk(tb[bi])[0:n, KVW:2 * KVW]),
                         R=[bres(tb[bi])], W=["vtm%d" % bi])
                    is_last_prompt = last_pass and bi == NB - 1
                    is_samp = bi >= NB
                    if is_last_prompt or is_samp:
                        kvs = kv32[bi % 2]
                        P.op("dve", lambda e, bi=bi, n=n, kvs=kvs: e.tensor_copy(out=kvs[0:n, :], in_=Bk(tb[bi])[0:n, 0:2 * KVW]),
                             R=[bres(tb[bi])], W=["kv32_%d" % (bi % 2)])
                        ko, vo = (sko[l], svo[l]) if is_samp else (pk[l], pv[l])
                        dma("sp", ko, kvs[0:n, 0:KVW], R=["kv32_%d" % (bi % 2)], W=[], key="okv")
                        dma("sp", vo, kvs[0:n, KVW:2 * KVW], R=["kv32_%d" % (bi % 2)], W=[], key="okv")

                dbg("kdup_p%d_l%d" % (ps, l), kdup, "kdup", [128, NKV, TT])
                dbg("q_p%d_l%d" % (ps, l), qu[:, 0:QC, :], "qu", [128, QC, TT])
                for p0 in range(0, UC, 4):
                    def ev_u(i, b, p0=p0):
                        P.op("act", lambda e: e.activation(out=qu[:, QC + p0 + i, :], in_=Bk(b)[:, 0:TT], func=AF.Gelu_apprx_tanh),
                             R=[bres(b)], W=["qu"])
                    fm_panel(h, "h", 4, ev_u, bset())

                NGP = GM // 512
                for gp in range(NGP):
                    bs_ = bset()
                    for k0 in range(0, KC, 8):
                        wt, wres = w_next(8, 512)
                        for kl in range(8):
                            kc = k0 + kl
                            for bi in range(NBLK):
                                n = blk_rows(bi); c0, _ = blk_cols(bi)
                                P.op("pe", lambda e, wt=wt, kl=kl, kc=kc, bi=bi, n=n, c0=c0, bs_=bs_: e.matmul(
                                    Bk(bs_[bi])[0:n, 0:512], lhsT=h[:, kc, c0:c0 + n], rhs=wt[:, kl, :],
                                    start=(kc == 0), stop=(kc == KC - 1)), R=[wres, "h"], W=[bres(bs_[bi])])
                    for bi in range(NBLK):
                        n = blk_rows(bi)
                        dst = gv32 if bi >= NB else gv[bi]
                        dres = "gv32" if bi >= NB else "gv%d" % bi
                        P.op("act", lambda e, bi=bi, n=n, dst=dst, gp=gp, bs_=bs_: e.activation(
                            out=dst[0:n, gp * 512:(gp + 1) * 512], in_=Bk(bs_[bi])[0:n, 0:512], func=AF.Gelu_apprx_tanh),
                            R=[bres(bs_[bi])], W=[dres])
                        P.op("act", lambda e, bi=bi, n=n, dst=dst, gp=gp: e.activation(
                            out=tmp[0:n, 0:512], in_=dst[0:n, gp * 512:(gp + 1) * 512], func=AF.Square,
                            accum_out=ss[0:n, bi * 8 + gp: bi * 8 + gp + 1]), R=[dres, "ss"], W=["tmp", "ss"])
                for bi in range(NBLK):
                    n = blk_rows(bi)
                    dst = gv32 if bi >= NB else gv[bi]
                    dres = "gv32" if bi >= NB else "gv%d" % bi
                    sc_ = ss[0:n, bi * 8 + 4: bi * 8 + 5]
                    sc2 = ss[0:n, bi * 8 + 5: bi * 8 + 6]
                    sc3 = ss[0:n, bi * 8 + 6: bi * 8 + 7]
                    P.op("dve", lambda e, bi=bi, n=n, sc_=sc_: e.reduce_sum(out=sc_, in_=ss[0:n, bi * 8: bi * 8 + NGP], axis=AX.X),
                         R=["ss"], W=["ss"])
                    P.op("act", lambda e, n=n, sc_=sc_, sc2=sc2: e.activation(out=sc2, in_=sc_, func=AF.Sqrt, bias=epsb[0:n, 0:1],
                                                                               scale=1.0 / GM), R=["ss", "eps"], W=["ss"])
                    P.op("dve", lambda e, sc2=sc2, sc3=sc3: e.reciprocal(out=sc3, in_=sc2), R=["ss"], W=["ss"])
                    P.op("dve", lambda e, n=n, dst=dst, sc3=sc3: e.scalar_tensor_tensor(
                        out=dst[0:n, :], in0=dst[0:n, :], scalar=sc3, in1=ggm_bc[0:n, :], op0=ALU.mult, op1=ALU.mult),
                        R=[dres, "ss", "ggm_bc"], W=[dres])
                    if bi >= NB:
                        dma("sp", sgo[l], gv32[0:n, :], R=["gv32"], W=[], key="osg")
                        P.op("act", lambda e, bi=bi, n=n: e.copy(out=gv[bi][0:n, :], in_=gv32[0:n, :]), R=["gv32"], W=["gv%d" % bi])

                pcnt = [0]
                scnt = [0]

                def attention_unit(c0, nt, sources, ures):
                    for j in range(NKV):
                        po = 4 + 2 * (pcnt[0] % 2)
                        pd = po + 1
                        pcnt[0] += 1
                        for par in range(2):
                            prs = slice(par * 64, (par + 1) * 64)
                            pbs = []
                            for si, (kd, kres, vv, vres, nk, mk, mres) in enumerate(sources):
                                sb = scnt[0] % 4
                                scnt[0] += 1
                                pbuf = Pb[(par * 3 + si) % 6] if len(sources) <= 3 else Pb[si % 6]
                                pres = "P%d" % ((par * 3 + si) % 6 if len(sources) <= 3 else si % 6)
                                P.op("pe", lambda e, sb=sb, kd=kd, nk=nk, prs=prs, j=j: e.matmul(
                                    Bk(sb)[0:nk, 0:CPK * nt].rearrange("p (c t) -> p c t", t=nt),
                                    lhsT=kd(j)[prs, 0:nk], rhs=qu[prs, j * CPK:(j + 1) * CPK, c0:c0 + nt],
                                    start=True, stop=True), R=[kres, "qu"], W=[bres(sb)])
                                P.op("act", lambda e, sb=sb, nk=nk, pbuf=pbuf: e.activation(
                                    out=pbuf[0:nk, 0:CPK * nt], in_=Bk(sb)[0:nk, 0:CPK * nt], func=AF.Exp),
                                    R=[bres(sb)], W=[pres])
                                P.op("dve", lambda e, nk=nk, pbuf=pbuf, mk=mk: e.tensor_tensor(
                                    out=pbuf[0:nk, 0:CPK * nt].rearrange("p (c t) -> p c t", t=nt),
                                    in0=pbuf[0:nk, 0:CPK * nt].rearrange("p (c t) -> p c t", t=nt),
                                    in1=mk.unsqueeze(1).to_broadcast([nk, CPK, nt]), op=ALU.mult),
                                    R=[pres, mres], W=[pres])
                                pbs.append((pbuf, pres, vv, vres, nk))
                            for si, (pbuf, pres, vv, vres, nk) in enumerate(pbs):
                                P.op("pe", lambda e, po=po, prs=prs, vv=vv, nk=nk, pbuf=pbuf, si=si, j=j: e.matmul(
                                    Bk(po)[prs, 0:CPK * nt], lhsT=vv[0:nk, j * 64:(j + 1) * 64], rhs=pbuf[0:nk, 0:CPK * nt],
                                    start=(si == 0), stop=(si == len(pbs) - 1)), R=[pres, vres], W=[bres(po)])
                            for si, (pbuf, pres, vv, vres, nk) in enumerate(pbs):
                                P.op("pe", lambda e, pd=pd, prs=prs, nk=nk, pbuf=pbuf, si=si: e.matmul(
                                    Bk(pd)[prs, 0:CPK * nt], lhsT=ones_b[0:nk, 0:64], rhs=pbuf[0:nk, 0:CPK * nt],
                                    start=(si == 0), stop=(si == len(pbs) - 1)), R=[pres, "ones"], W=[bres(pd)])
                        for i in range(CPK):
                            cidx = l * QC + j * CPK + i
                            P.op("dve", lambda e, pd=pd, i=i, cidx=cidx: e.tensor_scalar(
                                out=rt[:, i * nt:(i + 1) * nt], in0=Bk(pd)[:, i * nt:(i + 1) * nt],
                                scalar1=sinkexp[:, cidx:cidx + 1], scalar2=None, op0=ALU.add),
                                R=[bres(pd), "sinkexp"], W=["rt"])
                        P.op("dve", lambda e: e.reciprocal(out=rt[:, 0:CPK * nt], in_=rt[:, 0:CPK * nt]), R=["rt"], W=["rt"])
                        P.op("dve", lambda e, po=po, j=j: e.tensor_tensor(
                            out=qu[:, j * CPK:(j + 1) * CPK, c0:c0 + nt],
                            in0=Bk(po)[:, 0:CPK * nt].rearrange("p (c t) -> p c t", t=nt),
                            in1=rt[:, 0:CPK * nt].rearrange("p (c t) -> p c t", t=nt), op=ALU.mult),
                            R=[bres(po), "rt"], W=["qu"])

                for bi in range(NB):
                    srcs = []
                    first_of_seq = (ps == 0 and bi == 0)
                    if not first_of_seq:
                        if bi == 0:
                            srcs.append((lambda j: kcar[l][:, j, :], "kcar%d" % l, vcar[l], "vcar%d" % l, 128, mprev, "mprev"))
                        else:
                            srcs.append((lambda j, bi=bi: kdup[:, j, (bi - 1) * 128: bi * 128], "kdup", vtm[bi - 1],
                                         "vtm%d" % (bi - 1), 128, mprev, "mprev"))
                    srcs.append((lambda j, bi=bi: kdup[:, j, bi * 128:(bi + 1) * 128], "kdup", vtm[bi], "vtm%d" % bi, 128, mcur, "mcur"))
                    attention_unit(bi * 128, 128, srcs, None)
                if has_s:
                    srcs = []
                    for bq in range(NS):
                        srcs.append((lambda j, bq=bq: khd[bq][:, j, :], "khd%d" % bq, vh[bq], "vh%d" % bq, 128, mhist[:, bq, :], "mhist"))
                    srcs.append((lambda j: kdup[:, j, T:T + S], "kdup", vtm[NB], "vtm%d" % NB, S, mnew[0:S, :], "mnew"))
                    attention_unit(T, S, srcs, None)
                if not last_pass:
                    P.op("act", lambda e: e.copy(out=kcar[l], in_=kdup[:, :, T - 128:T]), R=["kdup"], W=["kcar%d" % l])
                    P.op("act", lambda e: e.copy(out=vcar[l], in_=vtm[NB - 1]), R=["vtm%d" % (NB - 1)], W=["vcar%d" % l])

                for bi in range(NBLK):
                    n = blk_rows(bi); c0, nt = blk_cols(bi)
                    for hg in range(0, GH, 4):
                        b = nbank() % 4
                        for i in range(4):
                            hd = hg + i
                            rhs = wsT[:, hd, :] if bi < NB else wsamp[0:S, hd, :]
                            P.op("pe", lambda e, b=b, i=i, hd=hd, bi=bi, n=n, nt=nt, rhs=rhs: e.matmul(
                                Bk(b)[:, i * nt:(i + 1) * nt], lhsT=gv[bi][0:n, hd * 128:(hd + 1) * 128], rhs=rhs,
                                start=True, stop=True), R=["gv%d" % bi, "wsT", "wsamp"], W=[bres(b)])
                        if bi < NB:
                            bias_v = bbc[:, hg:hg + 4, :]
                            t3 = tmp[:, 0:4 * nt].rearrange("p (c t) -> p c t", t=nt)
                            b3 = Bk(b)[:, 0:4 * nt].rearrange("p (c t) -> p c t", t=nt)
                        else:
                            bias_v = bbc[:, hg:hg + 4, 0:8].unsqueeze(2).to_broadcast([128, 4, NS, 8])
                            t3 = tmp[:, 0:4 * nt].rearrange("p (c b t) -> p c b t", b=NS, t=8)
                            b3 = Bk(b)[:, 0:4 * nt].rearrange("p (c b t) -> p c b t", b=NS, t=8)
                        P.op("dve", lambda e, t3=t3, b3=b3, bias_v=bias_v: e.tensor_tensor(out=t3, in0=b3, in1=bias_v, op=ALU.add),
                             R=[bres(b), "bbc"], W=["tmp"])
                        uq = qu[:, QC + hg:QC + hg + 4, c0:c0 + nt]
                        P.op("dve", lambda e, uq=uq, nt=nt: e.tensor_tensor(
                            out=uq, in0=uq, in1=tmp[:, 0:4 * nt].rearrange("p (c t) -> p c t", t=nt), op=ALU.mult),
                            R=["tmp", "qu"], W=["qu"])

                dbg("mix_p%d_l%d" % (ps, l), qu, "qu", [128, KC, TT])
                fence()
                for p0 in range(0, KC, 4):
                    def ev_m(i, b, p0=p0):
                        P.op("act", lambda e: e.copy(out=my[:, p0 + i, :], in_=Bk(b)[:, 0:TT]), R=[bres(b)],
                             W=["my"])
                    fm_panel(qu, "qu", 4, ev_m, bset())
                dbg("mym_p%d_l%d" % (ps, l), my, "my", [128, KC, TT])
                postnorm(l, "post_mix")
                dbg("xmid_p%d_l%d" % (ps, l), x, "x", [128, KC, TT])

                prenorm(l, "pre_ffn")
                fence()
                abuf, _ = carve(QU_BASE, [128, RND, TT], BF16)
                ft = Bump(MY_END, ARENA_WORDS)
                gext = ft([128, 4, TTX], F32)
                cbuf = [ft([128, TT], F32) for _ in range(4)]
                gel = [ft([128, TT], F32) for _ in range(4)]
                if has_s:
                    scs = ft([128, NS * 2 * FC], F32)
                    rows = sc[l].rearrange("b j (f p) -> (b j f) p", p=128)
                    for r0 in range(0, NS * 2 * FC, 128):
                        n = min(128, NS * 2 * FC - r0)
                        tload(scs[:, r0:r0 + n], "scs", rows[r0:r0 + n, :], n)
                    scs4 = scs.rearrange("p (b j f) -> p f b j", b=NS, j=2)
                    gtail = ft([128, FC, NS * 2], F32)
                    ostg = ft([128, 512], F32)
                if last_pass:
                    ptail = ft([128, FC, 2], F32)
                    ostg2 = ft([128, 512], F32)

                for r0 in range(0, FC, RND):
                    r1 = min(FC, r0 + RND)
                    for f0 in range(r0, r1, 4):
                        n = min(4, r1 - f0)
                        gb = [0, 1, 2, 3][:n]
                        ub = [4, 5, 6, 7][:n]
                        for (bset_, tag) in ((gb, "g"), (ub, "u")):
                            for k0 in range(0, KC, 8):
                                wt, wres = w_next(8, n * 128)
                                for kl in range(8):
                                    kc = k0 + kl
                                    for i in range(n):
                                        b = bset_[i]
                                        P.op("pe", lambda e, wt=wt, kl=kl, i=i, kc=kc, b=b: e.matmul(
                                            Bk(b)[:, 0:TT], lhsT=wt[:, kl, i * 128:(i + 1) * 128], rhs=h[:, kc, :],
                                            start=(kc == 0), stop=(kc == KC - 1)), R=[wres, "h"], W=[bres(b)])
                        P.op("dve", lambda e, f0=f0, n=n: e.tensor_copy(out=gext[:, 0:n, 0:2], in_=convcar[l][:, f0:f0 + n, :]),
                             R=["convcar%d" % l], W=["gext"])
                        if has_s:
                            ge_s = gext[:, :, 2 + T:2 + T + NS * 10].rearrange("p c (b t) -> p c b t", t=10)
                            P.op("dve", lambda e, f0=f0, n=n, ge_s=ge_s: e.tensor_copy(out=ge_s[:, 0:n, :, 0:2], in_=scs4[:, f0:f0 + n, :, :]),
                                 R=["scs"], W=["gext"])
                        for i in range(n):
                            P.op("act", lambda e, i=i, b=gb[i]: e.copy(out=gext[:, i, 2:2 + T], in_=Bk(b)[:, 0:T]), R=[bres(gb[i])], W=["gext"])
                            if has_s:
                                P.op("act", lambda e, i=i, ge_s=ge_s, b=gb[i]: e.copy(
                                    out=ge_s[:, i, :, 2:10], in_=Bk(b)[:, T:T + S].rearrange("p (b t) -> p b t", t=8)),
                                    R=[bres(gb[i])], W=["gext"])
                        if not last_pass:
                            P.op("dve", lambda e, f0=f0, n=n: e.tensor_copy(out=convcar[l][:, f0:f0 + n, :], in_=gext[:, 0:n, T:T + 2]),
                                 R=["gext"], W=["convcar%d" % l])
                        else:
                            P.op("dve", lambda e, f0=f0, n=n: e.tensor_copy(out=ptail[:, f0:f0 + n, :], in_=gext[:, 0:n, T:T + 2]),
                                 R=["gext"], W=["ptail"])
                        if has_s:
                            P.op("dve", lambda e, f0=f0, n=n, ge_s=ge_s: e.tensor_copy(
                                out=gtail[:, f0:f0 + n, :].rearrange("p f (b j) -> p f b j", j=2), in_=ge_s[:, 0:n, :, 8:10]),
                                R=["gext"], W=["gtail"])
                        for j3 in range(3):
                            for i in range(n):
                                fc = f0 + i
                                wcol = cwt[:, (l * 3 + j3) * FC + fc:(l * 3 + j3) * FC + fc + 1]
                                bcol = cbt[:, l * FC + fc:l * FC + fc + 1]
                                cbi = cbuf[i]
                                if j3 == 0:
                                    P.op("dve", lambda e, i=i, wcol=wcol, bcol=bcol, cbi=cbi: e.tensor_scalar(
                                        out=cbi[:, 0:T], in0=gext[:, i, 0:T], scalar1=wcol, scalar2=bcol, op0=ALU.mult, op1=ALU.add),
                                        R=["gext", "cwt", "cbt"], W=["cbuf%d" % i])
                                    if has_s:
                                        P.op("dve", lambda e, i=i, wcol=wcol, bcol=bcol, cbi=cbi, ge_s=ge_s: e.tensor_scalar(
                                            out=cbi[:, T:T + S].rearrange("p (b t) -> p b t", t=8), in0=ge_s[:, i, :, 0:8],
                                            scalar1=wcol, scalar2=bcol, op0=ALU.mult, op1=ALU.add),
                                            R=["gext", "cwt", "cbt"], W=["cbuf%d" % i])
                                else:
                                    P.op("dve", lambda e, i=i, wcol=wcol, cbi=cbi, j3=j3: e.scalar_tensor_tensor(
                                        out=cbi[:, 0:T], in0=gext[:, i, j3:j3 + T], scalar=wcol, in1=cbi[:, 0:T],
                                        op0=ALU.mult, op1=ALU.add), R=["gext", "cwt", "cbuf%d" % i], W=["cbuf%d" % i])
                                    if has_s:
                                        P.op("dve", lambda e, i=i, wcol=wcol, cbi=cbi, j3=j3, ge_s=ge_s: e.scalar_tensor_tensor(
                                            out=cbi[:, T:T + S].rearrange("p (b t) -> p b t", t=8), in0=ge_s[:, i, :, j3:j3 + 8],
                                            scalar=wcol, in1=cbi[:, T:T + S].rearrange("p (b t) -> p b t", t=8),
                                            op0=ALU.mult, op1=ALU.add), R=["gext", "cwt", "cbuf%d" % i], W=["cbuf%d" % i])
                        for i in range(n):
                            P.op("act", lambda e, i=i: e.activation(out=gel[i], in_=cbuf[i], func=AF.Gelu_apprx_tanh),
                                 R=["cbuf%d" % i], W=["gel%d" % i])
                            P.op("dve", lambda e, i=i, f0=f0, r0=r0, b=ub[i]: e.tensor_tensor(
                                out=abuf[:, f0 - r0 + i, :], in0=gel[i], in1=Bk(b)[:, 0:TT], op=ALU.mult),
                                R=["gel%d" % i, bres(ub[i])], W=["qu"])
                    nr = r1 - r0
                    for c0 in range(0, D, 512):
                        bs_ = bset()
                        for t0 in range(0, nr, 8):
                            n8 = min(8, nr - t0)
                            wt, wres = w_next(n8, 512)
                            for fl in range(n8):
                                fa = t0 + fl
                                for i in range(4):
                                    b = bs_[i]
                                    P.op("pe", lambda e, wt=wt, fl=fl, fa=fa, i=i, b=b, nr=nr: e.matmul(
                                        Bk(b)[:, 0:TT], lhsT=wt[:, fl, i * 128:(i + 1) * 128], rhs=abuf[:, fa, :],
                                        start=(fa == 0), stop=(fa == nr - 1)), R=[wres, "qu"], W=[bres(b)])
                        for i in range(4):
                            dc = c0 // 128 + i
                            b = bs_[i]
                            if r0 == 0:
                                P.op("act", lambda e, dc=dc, b=b: e.copy(out=my[:, dc, :], in_=Bk(b)[:, 0:TT]), R=[bres(b)], W=["my"])
                            else:
                                P.op("dve", lambda e, dc=dc, b=b: e.tensor_tensor(out=my[:, dc, :], in0=my[:, dc, :], in1=Bk(b)[:, 0:TT],
                                                                                   op=ALU.add), R=[bres(b), "my"], W=["my"])
                def tail_out(src, srcres, ncol, dst_rows, stg_, sres, key):
                    for g0 in range(0, FC, 4):
                        n = min(4, FC - g0)
                        b = nbank()
                        for i in range(n):
                            P.op("pe", lambda e, b=b, i=i, g0=g0: e.transpose(Bk(b)[0:ncol, i * 128:(i + 1) * 128], src[:, g0 + i, :], ident),
                                 R=[srcres, "ident"], W=[bres(b)])
                        P.op("act", lambda e, b=b, n=n: e.copy(out=stg_[0:ncol, 0:n * 128], in_=Bk(b)[0:ncol, 0:n * 128]),
                             R=[bres(b)], W=[sres])
                        dma("sp", dst_rows[:, g0 * 128:(g0 + n) * 128], stg_[0:ncol, 0:n * 128], R=[sres], W=[], key=key)
                if has_s:
                    tail_out(gtail, "gtail", NS * 2, sco[l].rearrange("b j f -> (b j) f"), ostg, "ostg", "osc")
                if last_pass:
                    tail_out(ptail, "ptail", 2, pc[l], ostg2, "ostg2", "opc")
                dbg("myf_p%d_l%d" % (ps, l), my, "my", [128, KC, TT])
                postnorm(l, "post_ffn")
                dbg("xout_p%d_l%d" % (ps, l), x, "x", [128, KC, TT])

            for l in range(L):
                do_layer(l)

            fence()
            stg = Bump(my_base, ARENA_WORDS)
            xo = stg([128, D], F32)
            for bi in range(NBLK):
                n = blk_rows(bi); c0, _ = blk_cols(bi)
                for g0 in range(0, KC, 4):
                    b = nbank()
                    for i in range(4):
                        kc = g0 + i
                        P.op("pe", lambda e, b=b, i=i, kc=kc, n=n, c0=c0: e.transpose(
                            Bk(b)[0:n, i * 128:(i + 1) * 128], x[:, kc, c0:c0 + n], ident), R=["x", "ident"], W=[bres(b)])
                    P.op("act", lambda e, b=b, g0=g0, n=n: e.copy(out=xo[0:n, g0 * 128:(g0 + 4) * 128], in_=Bk(b)[0:n, 0:512]),
                         R=[bres(b)], W=["xo"])
                dst = yp[tok0 + bi * 128: tok0 + bi * 128 + 128, :] if bi < NB else ys[:, :]
                dma("sp", dst, xo[0:n, :], R=["xo"], W=[], key="oy")
            return T

        tok0 = 0
        for ps in range(NPASS):
            tok0 += do_pass(ps, tok0)

        assert wstate["next_use"] == len(tiles), (wstate, len(tiles))

    import contextlib
    with contextlib.ExitStack() as es:
        arena = es.enter_context(nc.sbuf_tensor("arena", [128, ARENA_WORDS], F32))
        banks = [es.enter_context(nc.psum_tensor("bank%d" % i, [128, 512], F32)) for i in range(8)]
        body(arena, banks)
        sems = {e: es.enter_context(nc.semaphore("s_" + e)) for e in Prog.ENGS}
        dsems = {k: es.enter_context(nc.semaphore("d_" + k)) for k in P.dma_cnt}
        block = es.enter_context(nc.Block())
        P.emit(nc, block, sems, dsems)
    return nc


def make_consts(c):
    S, NS = c["S"], c["NS"]
    s = np.arange(128)[:, None]; t = np.arange(128)[None, :]
    mcur = (s <= t).astype(np.float32)
    mprev = (s > t).astype(np.float32)
    mhist = np.zeros((128, NS, S), np.float32)
    for b in range(NS):
        mhist[:, b, b * 8:(b + 1) * 8] = mprev[:, 0:8]
    mnew = np.zeros((S, S), np.float32)
    for b in range(NS):
        mnew[b * 8:(b + 1) * 8, b * 8:(b + 1) * 8] = mcur[0:8, 0:8]
    return dict(c_ident=np.eye(128, dtype=np.float32), c_mcur=mcur, c_mprev=mprev, c_mhist=mhist, c_mnew=mnew)


def run(cfg, inputs):
    c = derive(cfg)
    NC, NS, S, L = c["NCORES"], c["NS"], c["S"], c["DEPTH"]
    B = c["BATCH"]
    f = lambda a: np.ascontiguousarray(np.asarray(a, dtype=np.float32))
    consts = make_consts(c)
    shared = dict(w_in=f(inputs["w_in"]), w_out=f(inputs["w_out"]), w_gate=f(inputs["w_ffn_gate"]),
                  w_up=f(inputs["w_ffn_up"]), w_down=f(inputs["w_ffn_down"]), sinks=f(inputs["attn_sinks"]),
                  gms=f(inputs["gm_spatial"]), gmb=f(inputs["gm_bias"]), gmn=f(inputs["gm_norm"]),
                  g_pre_mix=f(inputs["norm_pre_mix"]), g_post_mix=f(inputs["norm_post_mix"]),
                  g_pre_ffn=f(inputs["norm_pre_ffn"]), g_post_ffn=f(inputs["norm_post_ffn"]),
                  cw=f(inputs["conv_w"]), cb=f(inputs["conv_b"]), **consts)
    x_prompt = f(inputs["x_prompt"]); x_sample = f(inputs["x_sample"])
    ssk = f(inputs["state_swa_k"]); ssv = f(inputs["state_swa_v"]); ssc = f(inputs["state_conv"])
    CSEQ, SEQ, OWN_A = c["CSEQ"], c["SEQ"], c["OWN_A"]
    split = CSEQ < SEQ
    b_start = SEQ - CSEQ
    in_maps = []
    for core in range(NC):
        sl = slice(core * NS, (core + 1) * NS)
        m = dict(shared)
        if split:
            m["xp"] = np.ascontiguousarray(x_prompt[core // 2, (0 if core % 2 == 0 else b_start):][:CSEQ])
        else:
            m["xp"] = x_prompt[core % B]
        m["xs"] = np.ascontiguousarray(x_sample[sl].reshape(S, c["D"]))
        m["sk"] = np.ascontiguousarray(ssk[:, sl].reshape(L, NS, 128, c["KVW"]))
        m["sv"] = np.ascontiguousarray(ssv[:, sl].reshape(L, NS, 128, c["KVW"]))
        m["sc"] = np.ascontiguousarray(ssc[:, sl])
        in_maps.append(m)
    nc = build_nc(cfg)
    res = run_bass_kernel_spmd(nc, in_maps, core_ids=list(range(NC)))
    R = res.results
    if DEBUG:
        LAST_DBG.clear()
        LAST_DBG.update({k: v for k, v in R[0].items() if k.startswith("dbg_")})
    D, F, KVW, GM = c["D"], c["F"], c["KVW"], c["GM"]
    if split:
        y_prompt = np.stack([np.concatenate([R[2 * b]["yp"][:OWN_A], R[2 * b + 1]["yp"][OWN_A - b_start:]], axis=0)
                             for b in range(B)]).astype(np.float32)
        last = [2 * b + 1 for b in range(B)]
    else:
        y_prompt = np.stack([R[b]["yp"] for b in range(B)]).astype(np.float32)
        last = list(range(B))
    y_sample = np.concatenate([R[k]["ys"].reshape(NS, 8, D) for k in range(NC)], axis=0)
    p_k = np.stack([R[b]["pk"] for b in last], axis=1).reshape(L, B, 128, 4, 64)
    p_v = np.stack([R[b]["pv"] for b in last], axis=1).reshape(L, B, 128, 4, 64)
    p_c = np.stack([R[b]["pc"] for b in last], axis=1)
    s_k = np.concatenate([R[k]["sko"].reshape(L, NS, 8, 4, 64) for k in range(NC)], axis=1)
    s_v = np.concatenate([R[k]["svo"].reshape(L, NS, 8, 4, 64) for k in range(NC)], axis=1)
    s_c = np.concatenate([R[k]["sco"] for k in range(NC)], axis=1)
    s_g = np.concatenate([R[k]["sgo"].reshape(L, NS, 8, c["GH"], 128) for k in range(NC)], axis=1)
    return (y_prompt, y_sample, p_k, p_v, p_c, s_k, s_v, s_c, s_g)


def kernel(**inputs):
    return run(FULL_CFG, inputs)
```

```python
import numpy as np
import concourse.bass as bass
import concourse.mybir as mybir
from concourse.bass_utils import run_bass_kernel_spmd

F32 = mybir.dt.float32
BF16 = mybir.dt.bfloat16
AF = mybir.ActivationFunctionType
ALU = mybir.AluOpType
AX = mybir.AxisListType

FULL_CFG = dict(D=4096, SEQ=2048, CSEQ=1152, BATCH=4, DEC_BATCH=32, DEC_SEQ=8, DEPTH=2,
                PASSES=[3, 3, 2, 1], NCORES=8)
EPS = 1e-6
DEBUG = False
LAST_DBG = {}
ARENA_WORDS = 52000
WSLOTS = 3
WSLOT_ELEMS = 4096


def derive(cfg):
    c = dict(cfg)
    D = c["D"]
    c["ATT"] = D // 2
    c["GM"] = D - c["ATT"]
    c["HD"] = 64
    c["NH"] = c["ATT"] // 64
    c["NKV"] = 4
    c["GQA"] = c["NH"] // c["NKV"]
    c["CPK"] = c["GQA"] // 2
    c["GH"] = c["GM"] // 128
    c["F"] = ((8 * D // 3 + 255) // 256) * 256
    c["KC"] = D // 128
    c["FC"] = c["F"] // 128
    c["QC"] = c["ATT"] // 128
    c["UC"] = c["GM"] // 128
    c["KVW"] = c["NKV"] * 64
    c["IN"] = c["ATT"] + 2 * c["KVW"] + 2 * c["GM"]
    c["NS"] = c["DEC_BATCH"] // c["NCORES"]
    c["S"] = c["NS"] * c["DEC_SEQ"]
    c.setdefault("CSEQ", c["SEQ"])
    assert sum(c["PASSES"]) * 128 == c["CSEQ"]
    if c["CSEQ"] < c["SEQ"]:
        assert c["NCORES"] == 2 * c["BATCH"]
        assert c["SEQ"] - c["CSEQ"] >= 128 and c["SEQ"] - c["CSEQ"] + 2 * 128 <= c["CSEQ"]
    assert c["KC"] % 8 == 0 and c["CPK"] >= 1 and c["DEC_SEQ"] == 8 and c["NS"] == 4
    return c


class Prog:
    ENGS = ("pe", "act", "dve", "pool", "sp")

    def __init__(self):
        self.ops = {e: [] for e in self.ENGS}
        self.res = {}
        self.dma_cnt = {}

    def op(self, eng, fn, R=(), W=(), key=None):
        lst = self.ops[eng]
        me = (eng, len(lst))
        deps = set()
        for r in R:
            st = self.res.get(r)
            if st and st[0] is not None:
                deps.add(st[0])
        for w in W:
            st = self.res.get(w)
            if st:
                if st[0] is not None and st[0][0] != eng:
                    deps.add(st[0])
                for rd in st[1]:
                    if rd[0] != eng:
                        deps.add(rd)
        deps.discard(me)
        rec = dict(fn=fn, deps=deps, key=key, val=None, needed=False)
        if key is not None:
            self.dma_cnt[key] = self.dma_cnt.get(key, 0) + 1
            rec["val"] = 16 * self.dma_cnt[key]
        lst.append(rec)
        for d in deps:
            self.ops[d[0]][d[1]]["needed"] = True
        for r in R:
            self.res.setdefault(r, [None, []])[1].append(me)
        for w in W:
            self.res[w] = [me, []]
        return me

    def emit(self, nc, block, sems, dsems):
        cum = {}
        for e in self.ENGS:
            c = 0
            arr = []
            for rec in self.ops[e]:
                if rec["needed"] and rec["key"] is None:
                    c += 1
                arr.append(c)
            cum[e] = arr

        def target(dep):
            rec = self.ops[dep[0]][dep[1]]
            if rec["key"] is not None:
                return dsems[rec["key"]], rec["val"]
            return sems[dep[0]], cum[dep[0]][dep[1]]

        def run(e, handle):
            waited = {}
            for rec in self.ops[e]:
                need = {}
                for d in rec["deps"]:
                    s, v = target(d)
                    k = id(s)
                    if waited.get(k, (None, 0))[1] >= v:
                        continue
                    if k not in need or need[k][1] < v:
                        need[k] = (s, v)
                for k, (s, v) in need.items():
                    handle.wait_ge(s, v)
                    waited[k] = (s, v)
                ins = rec["fn"](handle)
                if rec["key"] is not None:
                    ins.then_inc(dsems[rec["key"]], 16)
                elif rec["needed"]:
                    ins.then_inc(sems[e], 1)
            done = {}
            for rec in self.ops[e]:
                if rec["key"] is not None:
                    done[rec["key"]] = max(done.get(rec["key"], 0), rec["val"])
            for k, v in done.items():
                handle.wait_ge(dsems[k], v)

        @block.tensor
        def _(t):
            run("pe", t)

        @block.scalar
        def _(a):
            run("act", a)

        @block.vector
        def _(v):
            run("dve", v)

        @block.gpsimd
        def _(g):
            run("pool", g)

        @block.sync
        def _(s):
            run("sp", s)


def build_nc(cfg):
    c = derive(cfg)
    D, KC, FC, F, QC, UC, GH = c["D"], c["KC"], c["FC"], c["F"], c["QC"], c["UC"], c["GH"]
    NKV, CPK, KVW, IN, GM, ATT = c["NKV"], c["CPK"], c["KVW"], c["IN"], c["GM"], c["ATT"]
    L, SEQ, S, NS = c["DEPTH"], c["CSEQ"], c["S"], c["NS"]
    PASSES = c["PASSES"]
    NPASS = len(PASSES)
    QSCALE = 64 ** -0.5
    RND = min(16, c["KC"])

    nc = bass.Bass("TRN2", target_bir_lowering=False)

    def din(name, shape):
        return nc.dram_tensor(name, list(shape), F32, kind="ExternalInput").ap()

    def dout(name, shape):
        return nc.dram_tensor(name, list(shape), F32, kind="ExternalOutput").ap()

    xp = din("xp", [SEQ, D]); xs = din("xs", [S, D])
    xprev = din("xprev", [128, D]); c_flag = din("c_flag", [128, 2])
    sk = din("sk", [L, NS, 128, KVW]); sv = din("sv", [L, NS, 128, KVW])
    sc = din("sc", [L, NS, 2, F])
    w_in = din("w_in", [L, D, IN]); w_out = din("w_out", [L, D, D])
    w_gate = din("w_gate", [L, D, F]); w_up = din("w_up", [L, D, F]); w_down = din("w_down", [L, F, D])
    sinks = din("sinks", [L, c["NH"]]); gms = din("gms", [L, GH, 128, 128]); gmb = din("gmb", [L, GH, 128])
    gmn = din("gmn", [L, GM])
    g_pre_mix = din("g_pre_mix", [L, D]); g_post_mix = din("g_post_mix", [L, D])
    g_pre_ffn = din("g_pre_ffn", [L, D]); g_post_ffn = din("g_post_ffn", [L, D])
    cw = din("cw", [L, 3, F]); cb = din("cb", [L, F])
    c_ident = din("c_ident", [128, 128]); c_mcur = din("c_mcur", [128, 128]); c_mprev = din("c_mprev", [128, 128])
    c_mhist = din("c_mhist", [128, NS, S]); c_mnew = din("c_mnew", [S, S])

    yp = dout("yp", [SEQ, D]); ys = dout("ys", [S, D])
    pk = dout("pk", [L, 128, KVW]); pv = dout("pv", [L, 128, KVW]); pc = dout("pc", [L, 2, F])
    sko = dout("sko", [L, S, KVW]); svo = dout("svo", [L, S, KVW]); sco = dout("sco", [L, NS, 2, F])
    sgo = dout("sgo", [L, S, GM])

    P = Prog()
    ctx = {}
    dbg_list = []

    def dbg(name, ap, res, shape):
        if not DEBUG or name in [d[0] for d in dbg_list]:
            return
        t = nc.dram_tensor("dbg_" + name, list(shape), F32, kind="ExternalOutput").ap()
        dbg_list.append((name, shape))
        eng = "pool" if ap.dtype == BF16 else "sp"
        P.op(eng, lambda e: e.dma_start(out=t, in_=ap), R=[res], W=[], key="dbg_" + name)

    def body(arena, banks):
        def carve(off, shape, dt):
            n = int(np.prod(shape[1:]))
            if dt == BF16:
                assert n % 2 == 0
                words = n // 2
                v = arena[:, off:off + words].bitcast(BF16)
            else:
                words = n
                v = arena[:, off:off + words]
            if len(shape) == 3:
                v = v.rearrange("p (a b) -> p a b", b=shape[2])
            elif len(shape) == 4:
                v = v.rearrange("p (a b c) -> p a b c", b=shape[2], c=shape[3])
            return v, off + words

        class Bump:
            def __init__(self, base, limit):
                self.o = base; self.limit = limit

            def __call__(self, shape, dt):
                v, self.o = carve(self.o, shape, dt)
                assert self.o <= self.limit, ("arena overflow", self.o, self.limit)
                return v

        pers = Bump(0, ARENA_WORDS)
        ident = pers([128, 128], F32)
        ones_b = pers([128, 128], BF16)
        mcur = pers([128, 128], BF16); mprev = pers([128, 128], BF16)
        mhist = pers([128, NS, S], BF16); mnew = pers([128, S], BF16)
        gains = {nm: pers([128, L * KC], F32) for nm in ("pre_mix", "post_mix", "pre_ffn", "post_ffn")}
        cwt = pers([128, L * 3 * FC], F32)
        cbt = pers([128, L * FC], F32)
        convcar = [pers([128, FC, 2], F32) for _ in range(L)]
        kcar = [pers([128, NKV, 128], BF16) for _ in range(L)]
        vcar = [pers([128, KVW], BF16) for _ in range(L)]
        sinkexp = pers([128, L * QC], F32)
        rstd = pers([128, 512], F32); rtmp = pers([128, 512], F32)
        tst = [pers([128, 128], F32) for _ in range(2)]
        epsb = pers([128, 2], F32)
        wslots = [pers([128, WSLOT_ELEMS], BF16) for _ in range(WSLOTS)]
        fcell = pers([128, 2], F32)
        flag = pers([128, 2], F32)
        mprev0 = pers([128, 128], BF16)
        PERS_END = pers.o

        OVL = ["my", "qu", "pre_x", "pre_h", "pre_sq", "bc", "xstage", "xo", "kdup", "kv32_0", "kv32_1", "gv32", "ss", "ggm_bc", "bbc", "wsT", "rt", "tmp",
               "skst", "skd", "wsamp32", "wsamp", "gext", "scs", "gtail", "ostg", "ptail", "ostg2"] + \
              ["vtm%d" % i for i in range(5)] + ["gv%d" % i for i in range(5)] + ["P%d" % i for i in range(6)] + \
              ["khd%d" % i for i in range(4)] + ["vh%d" % i for i in range(4)] + \
              ["cbuf%d" % i for i in range(4)] + ["gel%d" % i for i in range(4)]

        def fence():
            P.op("dve", lambda e: e.memset(fcell, 0.0), W=OVL)

        bank_rr = [0]

        def nbank():
            b = bank_rr[0] % 8
            bank_rr[0] += 1
            return b

        def Bk(i):
            return banks[i]

        def bres(i):
            return "bank%d" % i

        def dma(eng, out, in_, R, W, key, slow=False):
            if slow:
                P.op(eng, lambda e: e.dma_start(out=out, in_=in_, allow_slow_non_contiguous=True), R=R, W=W, key=key)
            else:
                P.op(eng, lambda e: e.dma_start(out=out, in_=in_), R=R, W=W, key=key)

        tcount = [0]

        def tload(dst, dstres, src_rows, R_):
            i = tcount[0] % 2
            tcount[0] += 1
            st = tst[i]
            dma("sp", st[0:R_, :], src_rows, R=[], W=["tst%d" % i], key="tst%d" % i)
            b = nbank()
            P.op("pe", lambda e: e.transpose(Bk(b)[:, 0:R_], st[0:R_, :], ident[0:R_, 0:R_]),
                 R=["tst%d" % i, "ident"], W=[bres(b)])
            P.op("act", lambda e: e.copy(out=dst, in_=Bk(b)[:, 0:R_]), R=[bres(b)], W=[dstres])

        dma("sp", ident, c_ident, R=[], W=["ident"], key="c0")
        dma("pool", mcur, c_mcur, R=[], W=["mcur"], key="c1")
        dma("pool", mprev, c_mprev, R=[], W=["mprev"], key="c2")
        dma("pool", mhist, c_mhist, R=[], W=["mhist"], key="c3")
        dma("pool", mnew[0:S, :], c_mnew, R=[], W=["mnew"], key="c4")
        P.op("dve", lambda e: e.memset(ones_b, 1.0), W=["ones"])
        P.op("dve", lambda e: e.memset(epsb, EPS), W=["eps"])
        for l in range(L):
            P.op("dve", lambda e, l=l: e.memset(convcar[l], 0.0), W=["convcar%d" % l])
            P.op("dve", lambda e, l=l: e.memset(kcar[l], 0.0), W=["kcar%d" % l])
            P.op("dve", lambda e, l=l: e.memset(vcar[l], 0.0), W=["vcar%d" % l])
        for nm, src in (("pre_mix", g_pre_mix), ("post_mix", g_post_mix), ("pre_ffn", g_pre_ffn), ("post_ffn", g_post_ffn)):
            rows = src.rearrange("l (k p) -> (l k) p", p=128)
            for r0 in range(0, L * KC, 128):
                n = min(128, L * KC - r0)
                tload(gains[nm][:, r0:r0 + n], "g_" + nm, rows[r0:r0 + n, :], n)
        rows = cw.rearrange("l j (f p) -> (l j f) p", p=128)
        for r0 in range(0, L * 3 * FC, 128):
            n = min(128, L * 3 * FC - r0)
            tload(cwt[:, r0:r0 + n], "cwt", rows[r0:r0 + n, :], n)
        rows = cb.rearrange("l (f p) -> (l f) p", p=128)
        for r0 in range(0, L * FC, 128):
            n = min(128, L * FC - r0)
            tload(cbt[:, r0:r0 + n], "cbt", rows[r0:r0 + n, :], n)
        for l in range(L):
            for par in range(2):
                src = sinks[l].rearrange("(c two) -> c two", two=2)[:, par].partition_broadcast(64)
                dma("sp", sinkexp[par * 64:(par + 1) * 64, l * QC:(l + 1) * QC], src, R=[], W=["sinkexp"],
                    key="c5", slow=True)
        P.op("act", lambda e: e.activation(out=sinkexp, in_=sinkexp, func=AF.Exp), R=["sinkexp"], W=["sinkexp"])

        def plan_tiles():
            tiles = []
            wv0 = w_in[0].rearrange("(k p) c -> p k c", p=128)
            for k0 in range(0, KC, 8):
                tiles.append((wv0[:, k0:k0 + 8, ATT:ATT + 2 * KVW], 8, 2 * KVW))
            for _p in range(NPASS):
                for l in range(L):
                    wv = w_in[l].rearrange("(k p) c -> p k c", p=128)
                    cols = [(c0, 512) for c0 in range(0, ATT, 512)] + [(ATT, 2 * KVW)] + \
                           [(c0, 512) for c0 in range(ATT + 2 * KVW, IN, 512)]
                    for (c0, w) in cols:
                        for k0 in range(0, KC, 8):
                            tiles.append((wv[:, k0:k0 + 8, c0:c0 + w], 8, w))
                    wv = w_out[l].rearrange("(k p) c -> p k c", p=128)
                    for c0 in range(0, D, 512):
                        for k0 in range(0, KC, 8):
                            tiles.append((wv[:, k0:k0 + 8, c0:c0 + 512], 8, 512))
                    gv_ = w_gate[l].rearrange("(k p) c -> p k c", p=128)
                    uv_ = w_up[l].rearrange("(k p) c -> p k c", p=128)
                    dv_ = w_down[l].rearrange("(f p) c -> p f c", p=128)
                    for r0 in range(0, FC, RND):
                        r1 = min(FC, r0 + RND)
                        for f0 in range(r0, r1, 4):
                            n = min(4, r1 - f0)
                            for wvv in (gv_, uv_):
                                for k0 in range(0, KC, 8):
                                    tiles.append((wvv[:, k0:k0 + 8, f0 * 128:(f0 + n) * 128], 8, n * 128))
                        for c0 in range(0, D, 512):
                            for t0 in range(r0, r1, 8):
                                n = min(8, r1 - t0)
                                tiles.append((dv_[:, t0:t0 + n, c0:c0 + 512], n, 512))
            return tiles

        tiles = plan_tiles()
        wstate = dict(next_issue=0, next_use=0)

        def w_issue():
            i = wstate["next_issue"]
            if i >= len(tiles):
                return
            src, a, b = tiles[i]
            s = i % WSLOTS
            dst = wslots[s][:, 0:a * b].rearrange("p (a b) -> p a b", b=b)
            dma("pool", dst, src, R=[], W=["wslot%d" % s], key="w%d" % s)
            wstate["next_issue"] += 1

        def w_next(a, b):
            i = wstate["next_use"]
            while wstate["next_issue"] < min(len(tiles), i + WSLOTS):
                w_issue()
            src, ta, tb = tiles[i]
            assert (ta, tb) == (a, b), (i, ta, tb, a, b)
            s = i % WSLOTS
            wstate["next_use"] += 1
            return wslots[s][:, 0:a * b].rearrange("p (a b) -> p a b", b=b), "wslot%d" % s


        def do_prepass():
            pb = Bump(PERS_END, ARENA_WORDS)
            xq = pb([128, KC, 128], F32)
            hq = pb([128, KC, 128], BF16)
            sqq = pb([128, KC, 128], BF16)
            xst = pb([128, D], F32)
            dma("sp", flag, c_flag, R=[], W=["flag"], key="c6")
            P.op("dve", lambda e: e.tensor_scalar(out=mprev0, in0=mprev, scalar1=flag[:, 0:1], scalar2=None, op0=ALU.mult),
                 R=["mprev", "flag"], W=["mprev0"])
            dma("sp", xst, xprev, R=[], W=["xstage"], key="xin")
            for g0 in range(0, KC, 4):
                b = nbank()
                for i in range(4):
                    kc = g0 + i
                    P.op("pe", lambda e, b=b, i=i, kc=kc: e.transpose(Bk(b)[:, i * 128:(i + 1) * 128], xst[:, kc * 128:(kc + 1) * 128], ident),
                         R=["xstage", "ident"], W=[bres(b)])
                P.op("act", lambda e, b=b, g0=g0: e.copy(out=xq[:, g0:g0 + 4, :], in_=Bk(b)[:, :].rearrange("p (a t) -> p a t", t=128)),
                     R=[bres(b)], W=["pre_x"])
            for g0 in range(0, KC, 8):
                P.op("act", lambda e, g0=g0: e.activation(out=sqq[:, g0:g0 + 8, :], in_=xq[:, g0:g0 + 8, :], func=AF.Square),
                     R=["pre_x"], W=["pre_sq"])
            b = nbank()
            for kc in range(KC):
                P.op("pe", lambda e, kc=kc, b=b: e.matmul(Bk(b)[:, 0:128], lhsT=ones_b, rhs=sqq[:, kc, :], start=(kc == 0), stop=(kc == KC - 1)),
                     R=["pre_sq", "ones"], W=[bres(b)])
            P.op("act", lambda e, b=b: e.activation(out=rtmp[:, 0:128], in_=Bk(b)[:, 0:128], func=AF.Sqrt, bias=epsb[:, 0:1], scale=1.0 / D),
                 R=[bres(b), "eps"], W=["rtmp"])
            P.op("dve", lambda e: e.reciprocal(out=rstd[:, 0:128], in_=rtmp[:, 0:128]), R=["rtmp"], W=["rstd"])
            g = gains["pre_mix"]
            for kc in range(KC):
                P.op("dve", lambda e, kc=kc: e.scalar_tensor_tensor(out=hq[:, kc, :], in0=xq[:, kc, :], scalar=g[:, kc:kc + 1], in1=rstd[:, 0:128],
                                                                    op0=ALU.mult, op1=ALU.mult), R=["pre_x", "rstd", "g_pre_mix"], W=["pre_h"])
            kb = [4, 5, 6, 7][:NKV]
            for k0 in range(0, KC, 8):
                wt, wres = w_next(8, 2 * KVW)
                for kl in range(8):
                    kc = k0 + kl
                    P.op("pe", lambda e, wt=wt, kl=kl, kc=kc: e.matmul(Bk(0)[:, 0:2 * KVW], lhsT=hq[:, kc, :], rhs=wt[:, kl, :],
                                                                      start=(kc == 0), stop=(kc == KC - 1)), R=[wres, "pre_h"], W=[bres(0)])
                    for j in range(NKV):
                        for par in range(2):
                            P.op("pe", lambda e, wt=wt, kl=kl, kc=kc, j=j, par=par: e.matmul(
                                Bk(kb[j])[par * 64:(par + 1) * 64, 0:128], lhsT=wt[:, kl, j * 64:(j + 1) * 64], rhs=hq[:, kc, :],
                                start=(kc == 0), stop=(kc == KC - 1)), R=[wres, "pre_h"], W=[bres(kb[j])])
            for j in range(NKV):
                P.op("act", lambda e, j=j: e.copy(out=kcar[0][:, j, :], in_=Bk(kb[j])[:, 0:128]), R=[bres(kb[j])], W=["kcar0"])
            P.op("act", lambda e: e.copy(out=vcar[0], in_=Bk(0)[:, KVW:2 * KVW]), R=[bres(0)], W=["vcar0"])

        do_prepass()

        def do_pass(ps, tok0):
            NB = PASSES[ps]
            T = NB * 128
            has_s = (ps == NPASS - 1)
            last_pass = (ps == NPASS - 1)
            SS = S if has_s else 0
            TT = T + SS
            assert TT <= 512
            NBLK = NB + (1 if has_s else 0)
            TTX = 2 + T + (NS * 10 if has_s else 0)

            pb = Bump(PERS_END, ARENA_WORDS)
            x = pb([128, KC, TT], F32)
            h = pb([128, KC, TT], BF16)
            qu = pb([128, KC, TT], BF16)
            QU_BASE = pb.o - (KC * TT) // 2
            my_base = pb.o
            my = pb([128, KC, TT], F32)
            MY_END = pb.o

            def blk_rows(bi):
                return 128 if bi < NB else SS

            def blk_cols(bi):
                return (bi * 128, 128) if bi < NB else (T, SS)

            fence()
            stg = Bump(my_base, ARENA_WORDS)
            xstage = stg([128, D], F32)
            for bi in range(NBLK):
                n = blk_rows(bi); c0, _ = blk_cols(bi)
                src = xp[tok0 + bi * 128: tok0 + bi * 128 + 128, :] if bi < NB else xs[:, :]
                dma("sp", xstage[0:n, :], src, R=[], W=["xstage"], key="xin")
                for g0 in range(0, KC, 4):
                    b = nbank()
                    for i in range(4):
                        kc = g0 + i
                        P.op("pe", lambda e, b=b, i=i, kc=kc, n=n: e.transpose(
                            Bk(b)[:, i * 128:i * 128 + n], xstage[0:n, kc * 128:(kc + 1) * 128], ident[0:n, 0:n]),
                            R=["xstage", "ident"], W=[bres(b)])
                    src_v = Bk(b)[:, :].rearrange("p (a t) -> p a t", t=128)[:, :, 0:n]
                    P.op("act", lambda e, g0=g0, c0=c0, n=n, src_v=src_v: e.copy(out=x[:, g0:g0 + 4, c0:c0 + n], in_=src_v),
                         R=[bres(b)], W=["x"])

            dbg("xin_p%d" % ps, x, "x", [128, KC, TT])

            def stat_rstd(src, srcres, sq, sqres, nfeat_chunks, outres):
                for g0 in range(0, nfeat_chunks, 8):
                    P.op("act", lambda e, g0=g0: e.activation(out=sq[:, g0:g0 + 8, :], in_=src[:, g0:g0 + 8, :], func=AF.Square),
                         R=[srcres], W=[sqres])
                b = nbank()
                for kc in range(nfeat_chunks):
                    P.op("pe", lambda e, kc=kc, b=b: e.matmul(Bk(b)[:, 0:TT], lhsT=ones_b, rhs=sq[:, kc, :],
                                                              start=(kc == 0), stop=(kc == nfeat_chunks - 1)),
                         R=[sqres, "ones"], W=[bres(b)])
                P.op("act", lambda e, b=b: e.activation(out=rtmp[:, 0:TT], in_=Bk(b)[:, 0:TT], func=AF.Sqrt,
                                                        bias=epsb[:, 0:1], scale=1.0 / (nfeat_chunks * 128)),
                     R=[bres(b), "eps"], W=["rtmp"])
                P.op("dve", lambda e: e.reciprocal(out=rstd[:, 0:TT], in_=rtmp[:, 0:TT]), R=["rtmp"], W=[outres])

            def prenorm(l, gname):
                stat_rstd(x, "x", qu, "qu", KC, "rstd")
                g = gains[gname]
                for kc in range(KC):
                    P.op("dve", lambda e, kc=kc: e.scalar_tensor_tensor(
                        out=h[:, kc, :], in0=x[:, kc, :], scalar=g[:, l * KC + kc:l * KC + kc + 1], in1=rstd[:, 0:TT],
                        op0=ALU.mult, op1=ALU.mult), R=["x", "rstd", "g_" + gname], W=["h"])

            def postnorm(l, gname):
                stat_rstd(my, "my", h, "h", KC, "rstd")
                g = gains[gname]
                for kc in range(KC):
                    P.op("dve", lambda e, kc=kc: e.scalar_tensor_tensor(
                        out=my[:, kc, :], in0=my[:, kc, :], scalar=g[:, l * KC + kc:l * KC + kc + 1], in1=rstd[:, 0:TT],
                        op0=ALU.mult, op1=ALU.mult), R=["my", "rstd", "g_" + gname], W=["my"])
                for g0 in range(0, KC, 8):
                    P.op("dve", lambda e, g0=g0: e.tensor_tensor(out=x[:, g0:g0 + 8, :], in0=x[:, g0:g0 + 8, :],
                                                                 in1=my[:, g0:g0 + 8, :], op=ALU.add),
                         R=["x", "my"], W=["x"])

            def fm_panel(hsrc, hres, nchunks, evac, bankset):
                w = nchunks * 128
                for k0 in range(0, KC, 8):
                    wt, wres = w_next(8, w)
                    for kl in range(8):
                        kc = k0 + kl
                        for i in range(nchunks):
                            b = bankset[i]
                            P.op("pe", lambda e, wt=wt, kl=kl, i=i, kc=kc, b=b: e.matmul(
                                Bk(b)[:, 0:TT], lhsT=wt[:, kl, i * 128:(i + 1) * 128], rhs=hsrc[:, kc, :],
                                start=(kc == 0), stop=(kc == KC - 1)), R=[wres, hres], W=[bres(b)])
                for i in range(nchunks):
                    evac(i, bankset[i])

            def do_layer(l):
                prenorm(l, "pre_mix")
                dbg("h_p%d_l%d" % (ps, l), h, "h", [128, KC, TT])
                dbg("rstd_p%d_l%d" % (ps, l), rstd[:, 0:TT], "rstd", [128, TT])
                fence()

                mt = Bump(my_base, ARENA_WORDS)
                kdup = mt([128, NKV, TT], BF16)
                vtm = [mt([128, KVW], BF16) for _ in range(NBLK)]
                kv32 = [mt([128, 2 * KVW], F32)] * 2
                gv = [mt([128, GM], BF16) for _ in range(NBLK)]
                ss = mt([128, 8 * NBLK], F32)
                ggm_bc = mt([128, GM], F32)
                bbc = ggm_bc.rearrange("p (h t) -> p h t", t=128)
                wsT = mt([128, GH, 128], BF16)
                Pb = [mt([128, 512], BF16) for _ in range(5)] + [None]
                rt = mt([128, 512], F32)
                tmp = mt([128, 512], F32)
                if has_s:
                    gv32 = mt([128, GM], F32)
                    khd = [mt([128, NKV, 128], BF16) for _ in range(NS)]
                    vh = [mt([128, KVW], BF16) for _ in range(NS)]
                    skst = rt[:, 0:KVW]
                    skd = tmp[:, 0:NKV * 128].rearrange("p (j a d) -> p j a d", a=2, d=64)
                    wsamp = mt([128, GH, S], BF16)

                dma("sp", ggm_bc, gmn[l].partition_broadcast(128), R=[], W=["bc"], key="bc0")
                for hd in range(GH):
                    i = tcount[0] % 2
                    tcount[0] += 1
                    st = tst[i]
                    dma("sp", st, gms[l, hd], R=[], W=["tst%d" % i], key="tst%d" % i)
                    b = nbank()
                    P.op("pe", lambda e, b=b, st=st: e.transpose(Bk(b)[:, 0:128], st, ident), R=["tst%d" % i, "ident"], W=[bres(b)])
                    P.op("dve", lambda e, b=b, hd=hd: e.tensor_tensor(out=wsT[:, hd, :], in0=Bk(b)[:, 0:128], in1=mcur, op=ALU.mult),
                         R=[bres(b), "mcur"], W=["wsT"])
                P.op("dve", lambda e: e.memset(ss, 0.0), W=["ss"])
                if has_s:
                    P.op("dve", lambda e: e.memset(wsamp, 0.0), W=["wsamp"])
                    for bq in range(NS):
                        dma("sp", wsamp[bq * 8:(bq + 1) * 8, :, bq * 8:(bq + 1) * 8], wsT[0:8, :, 0:8], R=["wsT"], W=["wsamp"],
                            key="wsamp")
                    for bq in range(NS):
                        dma("sp", skst, sk[l, bq], R=[], W=["rt"], key="skst")
                        P.op("dve", lambda e: e.tensor_copy(
                            out=skd, in_=skst.rearrange("p (j d) -> p j d", d=64).unsqueeze(2).to_broadcast([128, NKV, 2, 64])),
                            R=["rt"], W=["tmp"])
                        for j in range(NKV):
                            b = nbank()
                            P.op("pe", lambda e, b=b, j=j: e.transpose(Bk(b)[:, 0:128], skd[:, j].rearrange("p a d -> p (a d)"), ident),
                                 R=["tmp", "ident"], W=[bres(b)])
                            P.op("act", lambda e, b=b, j=j, bq=bq: e.copy(out=khd[bq][:, j, :], in_=Bk(b)[:, 0:128]),
                                 R=[bres(b)], W=["khd%d" % bq])
                        dma("pool", vh[bq], sv[l, bq], R=[], W=["vh%d" % bq], key="vh%d" % bq)

                pcount = [0]

                def bset():
                    s_ = [0, 1, 2, 3] if pcount[0] % 2 == 0 else [4, 5, 6, 7]
                    pcount[0] += 1
                    return s_

                for p0 in range(0, QC, 4):
                    def ev_q(i, b, p0=p0):
                        P.op("act", lambda e: e.mul(out=qu[:, p0 + i, :], in_=Bk(b)[:, 0:TT], mul=QSCALE),
                             R=[bres(b)], W=["qu"])
                    fm_panel(h, "h", 4, ev_q, bset())

                tb = list(range(NBLK))
                kb = [4, 5, 6, 7][:NKV]
                assert NBLK <= 4
                for k0 in range(0, KC, 8):
                    wt, wres = w_next(8, 2 * KVW)
                    for kl in range(8):
                        kc = k0 + kl
                        for bi in range(NBLK):
                            n = blk_rows(bi); c0, _ = blk_cols(bi)
                            P.op("pe", lambda e, wt=wt, kl=kl, kc=kc, bi=bi, n=n, c0=c0: e.matmul(
                                Bk(tb[bi])[0:n, 0:2 * KVW], lhsT=h[:, kc, c0:c0 + n], rhs=wt[:, kl, :],
                                start=(kc == 0), stop=(kc == KC - 1)), R=[wres, "h"], W=[bres(tb[bi])])
                        for j in range(NKV):
                            for par in range(2):
                                P.op("pe", lambda e, wt=wt, kl=kl, kc=kc, j=j, par=par: e.matmul(
                                    Bk(kb[j])[par * 64:(par + 1) * 64, 0:TT], lhsT=wt[:, kl, j * 64:(j + 1) * 64], rhs=h[:, kc, :],
                                    start=(kc == 0), stop=(kc == KC - 1)), R=[wres, "h"], W=[bres(kb[j])])
                for j in range(NKV):
                    P.op("act", lambda e, j=j: e.copy(out=kdup[:, j, :], in_=Bk(kb[j])[:, 0:TT]), R=[bres(kb[j])], W=["kdup"])
                for bi in range(NBLK):
                    n = blk_rows(bi)
                    P.op("act", lambda e, bi=bi, n=n: e.copy(out=vtm[bi][0:n, :], in_=Bk(tb[bi])[0:n, KVW:2 * KVW]),
                         R=[bres(tb[bi])], W=["vtm%d" % bi])
                    is_last_prompt = last_pass and bi == NB - 1
                    is_samp = bi >= NB
                    if is_last_prompt or is_samp:
                        kvs = kv32[bi % 2]
                        P.op("dve", lambda e, bi=bi, n=n, kvs=kvs: e.tensor_copy(out=kvs[0:n, :], in_=Bk(tb[bi])[0:n, 0:2 * KVW]),
                             R=[bres(tb[bi])], W=["kv32_0"])
                        ko, vo = (sko[l], svo[l]) if is_samp else (pk[l], pv[l])
                        dma("sp", ko, kvs[0:n, 0:KVW], R=["kv32_0"], W=[], key="okv")
                        dma("sp", vo, kvs[0:n, KVW:2 * KVW], R=["kv32_0"], W=[], key="okv")

                dbg("kdup_p%d_l%d" % (ps, l), kdup, "kdup", [128, NKV, TT])
                dbg("q_p%d_l%d" % (ps, l), qu[:, 0:QC, :], "qu", [128, QC, TT])
                for p0 in range(0, UC, 4):
                    def ev_u(i, b, p0=p0):
                        P.op("act", lambda e: e.activation(out=qu[:, QC + p0 + i, :], in_=Bk(b)[:, 0:TT], func=AF.Gelu_apprx_tanh),
                             R=[bres(b)], W=["qu"])
                    fm_panel(h, "h", 4, ev_u, bset())

                NGP = GM // 512
                for gp in range(NGP):
                    bs_ = bset()
                    for k0 in range(0, KC, 8):
                        wt, wres = w_next(8, 512)
                        for kl in range(8):
                            kc = k0 + kl
                            for bi in range(NBLK):
                                n = blk_rows(bi); c0, _ = blk_cols(bi)
                                P.op("pe", lambda e, wt=wt, kl=kl, kc=kc, bi=bi, n=n, c0=c0, bs_=bs_: e.matmul(
                                    Bk(bs_[bi])[0:n, 0:512], lhsT=h[:, kc, c0:c0 + n], rhs=wt[:, kl, :],
                                    start=(kc == 0), stop=(kc == KC - 1)), R=[wres, "h"], W=[bres(bs_[bi])])
                    for bi in range(NBLK):
                        n = blk_rows(bi)
                        dst = gv32 if bi >= NB else gv[bi]
                        dres = "gv32" if bi >= NB else "gv%d" % bi
                        P.op("act", lambda e, bi=bi, n=n, dst=dst, gp=gp, bs_=bs_: e.activation(
                            out=dst[0:n, gp * 512:(gp + 1) * 512], in_=Bk(bs_[bi])[0:n, 0:512], func=AF.Gelu_apprx_tanh),
                            R=[bres(bs_[bi])], W=[dres])
                        P.op("act", lambda e, bi=bi, n=n, dst=dst, gp=gp: e.activation(
                            out=tmp[0:n, 0:512], in_=dst[0:n, gp * 512:(gp + 1) * 512], func=AF.Square,
                            accum_out=ss[0:n, bi * 8 + gp: bi * 8 + gp + 1]), R=[dres, "ss"], W=["tmp", "ss"])
                for bi in range(NBLK):
                    n = blk_rows(bi)
                    dst = gv32 if bi >= NB else gv[bi]
                    dres = "gv32" if bi >= NB else "gv%d" % bi
                    sc_ = ss[0:n, bi * 8 + 4: bi * 8 + 5]
                    sc2 = ss[0:n, bi * 8 + 5: bi * 8 + 6]
                    sc3 = ss[0:n, bi * 8 + 6: bi * 8 + 7]
                    P.op("dve", lambda e, bi=bi, n=n, sc_=sc_: e.reduce_sum(out=sc_, in_=ss[0:n, bi * 8: bi * 8 + NGP], axis=AX.X),
                         R=["ss"], W=["ss"])
                    P.op("act", lambda e, n=n, sc_=sc_, sc2=sc2: e.activation(out=sc2, in_=sc_, func=AF.Sqrt, bias=epsb[0:n, 0:1],
                                                                               scale=1.0 / GM), R=["ss", "eps"], W=["ss"])
                    P.op("dve", lambda e, sc2=sc2, sc3=sc3: e.reciprocal(out=sc3, in_=sc2), R=["ss"], W=["ss"])
                    P.op("dve", lambda e, n=n, dst=dst, sc3=sc3: e.scalar_tensor_tensor(
                        out=dst[0:n, :], in0=dst[0:n, :], scalar=sc3, in1=ggm_bc[0:n, :], op0=ALU.mult, op1=ALU.mult),
                        R=[dres, "ss", "bc"], W=[dres])
                    if bi >= NB:
                        dma("sp", sgo[l], gv32[0:n, :], R=["gv32"], W=[], key="osg")
                        P.op("act", lambda e, bi=bi, n=n: e.copy(out=gv[bi][0:n, :], in_=gv32[0:n, :]), R=["gv32"], W=["gv%d" % bi])

                dma("sp", ggm_bc, gmb[l].rearrange("h t -> (h t)").partition_broadcast(128), R=[], W=["bc"], key="bc0")
                pcnt = [0]
                scnt = [0]

                def attention_unit(c0, nt, sources, ures):
                    for j in range(NKV):
                        po = 4 + 2 * (pcnt[0] % 2)
                        pd = po + 1
                        pcnt[0] += 1
                        for par in range(2):
                            prs = slice(par * 64, (par + 1) * 64)
                            pbs = []
                            for si, (kd, kres, vv, vres, nk, mk, mres) in enumerate(sources):
                                sb = scnt[0] % 4
                                scnt[0] += 1
                                pidx = (par * 2 + si) % 5 if len(sources) <= 2 else si % 5
                                pbuf = Pb[pidx]
                                pres = "P%d" % pidx
                                P.op("pe", lambda e, sb=sb, kd=kd, nk=nk, prs=prs, j=j: e.matmul(
                                    Bk(sb)[0:nk, 0:CPK * nt].rearrange("p (c t) -> p c t", t=nt),
                                    lhsT=kd(j)[prs, 0:nk], rhs=qu[prs, j * CPK:(j + 1) * CPK, c0:c0 + nt],
                                    start=True, stop=True), R=[kres, "qu"], W=[bres(sb)])
                                P.op("act", lambda e, sb=sb, nk=nk, pbuf=pbuf: e.activation(
                                    out=pbuf[0:nk, 0:CPK * nt], in_=Bk(sb)[0:nk, 0:CPK * nt], func=AF.Exp),
                                    R=[bres(sb)], W=[pres])
                                P.op("dve", lambda e, nk=nk, pbuf=pbuf, mk=mk: e.tensor_tensor(
                                    out=pbuf[0:nk, 0:CPK * nt].rearrange("p (c t) -> p c t", t=nt),
                                    in0=pbuf[0:nk, 0:CPK * nt].rearrange("p (c t) -> p c t", t=nt),
                                    in1=mk.unsqueeze(1).to_broadcast([nk, CPK, nt]), op=ALU.mult),
                                    R=[pres, mres], W=[pres])
                                pbs.append((pbuf, pres, vv, vres, nk))
                            for si, (pbuf, pres, vv, vres, nk) in enumerate(pbs):
                                P.op("pe", lambda e, po=po, prs=prs, vv=vv, nk=nk, pbuf=pbuf, si=si, j=j: e.matmul(
                                    Bk(po)[prs, 0:CPK * nt], lhsT=vv[0:nk, j * 64:(j + 1) * 64], rhs=pbuf[0:nk, 0:CPK * nt],
                                    start=(si == 0), stop=(si == len(pbs) - 1)), R=[pres, vres], W=[bres(po)])
                            for si, (pbuf, pres, vv, vres, nk) in enumerate(pbs):
                                P.op("pe", lambda e, pd=pd, prs=prs, nk=nk, pbuf=pbuf, si=si: e.matmul(
                                    Bk(pd)[prs, 0:CPK * nt], lhsT=ones_b[0:nk, 0:64], rhs=pbuf[0:nk, 0:CPK * nt],
                                    start=(si == 0), stop=(si == len(pbs) - 1)), R=[pres, "ones"], W=[bres(pd)])
                        for i in range(CPK):
                            cidx = l * QC + j * CPK + i
                            P.op("dve", lambda e, pd=pd, i=i, cidx=cidx: e.tensor_scalar(
                                out=rt[:, i * nt:(i + 1) * nt], in0=Bk(pd)[:, i * nt:(i + 1) * nt],
                                scalar1=sinkexp[:, cidx:cidx + 1], scalar2=None, op0=ALU.add),
                                R=[bres(pd), "sinkexp"], W=["rt"])
                        P.op("dve", lambda e: e.reciprocal(out=rt[:, 0:CPK * nt], in_=rt[:, 0:CPK * nt]), R=["rt"], W=["rt"])
                        P.op("dve", lambda e, po=po, j=j: e.tensor_tensor(
                            out=qu[:, j * CPK:(j + 1) * CPK, c0:c0 + nt],
                            in0=Bk(po)[:, 0:CPK * nt].rearrange("p (c t) -> p c t", t=nt),
                            in1=rt[:, 0:CPK * nt].rearrange("p (c t) -> p c t", t=nt), op=ALU.mult),
                            R=[bres(po), "rt"], W=["qu"])

                for bi in range(NB):
                    srcs = []
                    first_of_seq = (ps == 0 and bi == 0 and l > 0)
                    if not first_of_seq:
                        if bi == 0:
                            mk0, mk0r = (mprev0, "mprev0") if ps == 0 else (mprev, "mprev")
                            srcs.append((lambda j: kcar[l][:, j, :], "kcar%d" % l, vcar[l], "vcar%d" % l, 128, mk0, mk0r))
                        else:
                            srcs.append((lambda j, bi=bi: kdup[:, j, (bi - 1) * 128: bi * 128], "kdup", vtm[bi - 1],
                                         "vtm%d" % (bi - 1), 128, mprev, "mprev"))
                    srcs.append((lambda j, bi=bi: kdup[:, j, bi * 128:(bi + 1) * 128], "kdup", vtm[bi], "vtm%d" % bi, 128, mcur, "mcur"))
                    attention_unit(bi * 128, 128, srcs, None)
                if has_s:
                    srcs = []
                    for bq in range(NS):
                        srcs.append((lambda j, bq=bq: khd[bq][:, j, :], "khd%d" % bq, vh[bq], "vh%d" % bq, 128, mhist[:, bq, :], "mhist"))
                    srcs.append((lambda j: kdup[:, j, T:T + S], "kdup", vtm[NB], "vtm%d" % NB, S, mnew[0:S, :], "mnew"))
                    attention_unit(T, S, srcs, None)
                if not last_pass:
                    P.op("act", lambda e: e.copy(out=kcar[l], in_=kdup[:, :, T - 128:T]), R=["kdup"], W=["kcar%d" % l])
                    P.op("act", lambda e: e.copy(out=vcar[l], in_=vtm[NB - 1]), R=["vtm%d" % (NB - 1)], W=["vcar%d" % l])

                for bi in range(NBLK):
                    n = blk_rows(bi); c0, nt = blk_cols(bi)
                    for hg in range(0, GH, 4):
                        b = nbank() % 4
                        for i in range(4):
                            hd = hg + i
                            rhs = wsT[:, hd, :] if bi < NB else wsamp[0:S, hd, :]
                            P.op("pe", lambda e, b=b, i=i, hd=hd, bi=bi, n=n, nt=nt, rhs=rhs: e.matmul(
                                Bk(b)[:, i * nt:(i + 1) * nt], lhsT=gv[bi][0:n, hd * 128:(hd + 1) * 128], rhs=rhs,
                                start=True, stop=True), R=["gv%d" % bi, "wsT", "wsamp"], W=[bres(b)])
                        if bi < NB:
                            bias_v = bbc[:, hg:hg + 4, :]
                            t3 = tmp[:, 0:4 * nt].rearrange("p (c t) -> p c t", t=nt)
                            b3 = Bk(b)[:, 0:4 * nt].rearrange("p (c t) -> p c t", t=nt)
                        else:
                            bias_v = bbc[:, hg:hg + 4, 0:8].unsqueeze(2).to_broadcast([128, 4, NS, 8])
                            t3 = tmp[:, 0:4 * nt].rearrange("p (c b t) -> p c b t", b=NS, t=8)
                            b3 = Bk(b)[:, 0:4 * nt].rearrange("p (c b t) -> p c b t", b=NS, t=8)
                        P.op("dve", lambda e, t3=t3, b3=b3, bias_v=bias_v: e.tensor_tensor(out=t3, in0=b3, in1=bias_v, op=ALU.add),
                             R=[bres(b), "bc"], W=["tmp"])
                        uq = qu[:, QC + hg:QC + hg + 4, c0:c0 + nt]
                        P.op("dve", lambda e, uq=uq, nt=nt: e.tensor_tensor(
                            out=uq, in0=uq, in1=tmp[:, 0:4 * nt].rearrange("p (c t) -> p c t", t=nt), op=ALU.mult),
                            R=["tmp", "qu"], W=["qu"])

                dbg("mix_p%d_l%d" % (ps, l), qu, "qu", [128, KC, TT])
                fence()
                for p0 in range(0, KC, 4):
                    def ev_m(i, b, p0=p0):
                        P.op("act", lambda e: e.copy(out=my[:, p0 + i, :], in_=Bk(b)[:, 0:TT]), R=[bres(b)],
                             W=["my"])
                    fm_panel(qu, "qu", 4, ev_m, bset())
                dbg("mym_p%d_l%d" % (ps, l), my, "my", [128, KC, TT])
                postnorm(l, "post_mix")
                dbg("xmid_p%d_l%d" % (ps, l), x, "x", [128, KC, TT])

                prenorm(l, "pre_ffn")
                fence()
                abuf, _ = carve(QU_BASE, [128, RND, TT], BF16)
                ft = Bump(MY_END, ARENA_WORDS)
                qt = Bump(QU_BASE + (RND * TT) // 2, my_base)
                if qt.o + 4 * TTX + 3 * TT > my_base:
                    qt = ft
                gext = qt([128, 4, TTX], F32)
                cbuf = [qt([128, TT], F32) for _ in range(2)] * 2
                gel = [qt([128, TT], F32)] * 4
                if has_s:
                    scs = ft([128, NS * 2 * FC], F32)
                    rows = sc[l].rearrange("b j (f p) -> (b j f) p", p=128)
                    for r0 in range(0, NS * 2 * FC, 128):
                        n = min(128, NS * 2 * FC - r0)
                        tload(scs[:, r0:r0 + n], "scs", rows[r0:r0 + n, :], n)
                    scs4 = scs.rearrange("p (b j f) -> p f b j", b=NS, j=2)
                    gtail = ft([128, FC, NS * 2], F32)
                    ostg = ft([128, 256], F32)
                if last_pass:
                    ptail = ft([128, FC, 2], F32)
                    ostg2 = ostg if has_s else ft([128, 256], F32)

                for r0 in range(0, FC, RND):
                    r1 = min(FC, r0 + RND)
                    for f0 in range(r0, r1, 4):
                        n = min(4, r1 - f0)
                        gb = [0, 1, 2, 3][:n]
                        ub = [4, 5, 6, 7][:n]
                        for (bset_, tag) in ((gb, "g"), (ub, "u")):
                            for k0 in range(0, KC, 8):
                                wt, wres = w_next(8, n * 128)
                                for kl in range(8):
                                    kc = k0 + kl
                                    for i in range(n):
                                        b = bset_[i]
                                        P.op("pe", lambda e, wt=wt, kl=kl, i=i, kc=kc, b=b: e.matmul(
                                            Bk(b)[:, 0:TT], lhsT=wt[:, kl, i * 128:(i + 1) * 128], rhs=h[:, kc, :],
                                            start=(kc == 0), stop=(kc == KC - 1)), R=[wres, "h"], W=[bres(b)])
                        P.op("dve", lambda e, f0=f0, n=n: e.tensor_copy(out=gext[:, 0:n, 0:2], in_=convcar[l][:, f0:f0 + n, :]),
                             R=["convcar%d" % l], W=["gext"])
                        if has_s:
                            ge_s = gext[:, :, 2 + T:2 + T + NS * 10].rearrange("p c (b t) -> p c b t", t=10)
                            P.op("dve", lambda e, f0=f0, n=n, ge_s=ge_s: e.tensor_copy(out=ge_s[:, 0:n, :, 0:2], in_=scs4[:, f0:f0 + n, :, :]),
                                 R=["scs"], W=["gext"])
                        for i in range(n):
                            P.op("act", lambda e, i=i, b=gb[i]: e.copy(out=gext[:, i, 2:2 + T], in_=Bk(b)[:, 0:T]), R=[bres(gb[i])], W=["gext"])
                            if has_s:
                                P.op("act", lambda e, i=i, ge_s=ge_s, b=gb[i]: e.copy(
                                    out=ge_s[:, i, :, 2:10], in_=Bk(b)[:, T:T + S].rearrange("p (b t) -> p b t", t=8)),
                                    R=[bres(gb[i])], W=["gext"])
                        if not last_pass:
                            P.op("dve", lambda e, f0=f0, n=n: e.tensor_copy(out=convcar[l][:, f0:f0 + n, :], in_=gext[:, 0:n, T:T + 2]),
                                 R=["gext"], W=["convcar%d" % l])
                        else:
                            P.op("dve", lambda e, f0=f0, n=n: e.tensor_copy(out=ptail[:, f0:f0 + n, :], in_=gext[:, 0:n, T:T + 2]),
                                 R=["gext"], W=["ptail"])
                        if has_s:
                            P.op("dve", lambda e, f0=f0, n=n, ge_s=ge_s: e.tensor_copy(
                                out=gtail[:, f0:f0 + n, :].rearrange("p f (b j) -> p f b j", j=2), in_=ge_s[:, 0:n, :, 8:10]),
                                R=["gext"], W=["gtail"])
                        for i0 in range(0, n, 2):
                          ii_ = list(range(i0, min(n, i0 + 2)))
                          for j3 in range(3):
                            for i in ii_:
                                fc = f0 + i
                                wcol = cwt[:, (l * 3 + j3) * FC + fc:(l * 3 + j3) * FC + fc + 1]
                                bcol = cbt[:, l * FC + fc:l * FC + fc + 1]
                                cbi = cbuf[i]
                                if j3 == 0:
                                    P.op("dve", lambda e, i=i, wcol=wcol, bcol=bcol, cbi=cbi: e.tensor_scalar(
                                        out=cbi[:, 0:T], in0=gext[:, i, 0:T], scalar1=wcol, scalar2=bcol, op0=ALU.mult, op1=ALU.add),
                                        R=["gext", "cwt", "cbt"], W=["cbuf%d" % (i % 2)])
                                    if has_s:
                                        P.op("dve", lambda e, i=i, wcol=wcol, bcol=bcol, cbi=cbi, ge_s=ge_s: e.tensor_scalar(
                                            out=cbi[:, T:T + S].rearrange("p (b t) -> p b t", t=8), in0=ge_s[:, i, :, 0:8],
                                            scalar1=wcol, scalar2=bcol, op0=ALU.mult, op1=ALU.add),
                                            R=["gext", "cwt", "cbt"], W=["cbuf%d" % (i % 2)])
                                else:
                                    P.op("dve", lambda e, i=i, wcol=wcol, cbi=cbi, j3=j3: e.scalar_tensor_tensor(
                                        out=cbi[:, 0:T], in0=gext[:, i, j3:j3 + T], scalar=wcol, in1=cbi[:, 0:T],
                                        op0=ALU.mult, op1=ALU.add), R=["gext", "cwt", "cbuf%d" % (i % 2)], W=["cbuf%d" % (i % 2)])
                                    if has_s:
                                        P.op("dve", lambda e, i=i, wcol=wcol, cbi=cbi, j3=j3, ge_s=ge_s: e.scalar_tensor_tensor(
                                            out=cbi[:, T:T + S].rearrange("p (b t) -> p b t", t=8), in0=ge_s[:, i, :, j3:j3 + 8],
                                            scalar=wcol, in1=cbi[:, T:T + S].rearrange("p (b t) -> p b t", t=8),
                                            op0=ALU.mult, op1=ALU.add), R=["gext", "cwt", "cbuf%d" % (i % 2)], W=["cbuf%d" % (i % 2)])
                          for i in ii_:
                            P.op("act", lambda e, i=i: e.activation(out=gel[i], in_=cbuf[i], func=AF.Gelu_apprx_tanh),
                                 R=["cbuf%d" % (i % 2)], W=["gel0"])
                            P.op("dve", lambda e, i=i, f0=f0, r0=r0, b=ub[i]: e.tensor_tensor(
                                out=abuf[:, f0 - r0 + i, :], in0=gel[i], in1=Bk(b)[:, 0:TT], op=ALU.mult),
                                R=["gel0", bres(ub[i])], W=["qu"])
                    nr = r1 - r0
                    for c0 in range(0, D, 512):
                        bs_ = bset()
                        for t0 in range(0, nr, 8):
                            n8 = min(8, nr - t0)
                            wt, wres = w_next(n8, 512)
                            for fl in range(n8):
                                fa = t0 + fl
                                for i in range(4):
                                    b = bs_[i]
                                    P.op("pe", lambda e, wt=wt, fl=fl, fa=fa, i=i, b=b, nr=nr: e.matmul(
                                        Bk(b)[:, 0:TT], lhsT=wt[:, fl, i * 128:(i + 1) * 128], rhs=abuf[:, fa, :],
                                        start=(fa == 0), stop=(fa == nr - 1)), R=[wres, "qu"], W=[bres(b)])
                        for i in range(4):
                            dc = c0 // 128 + i
                            b = bs_[i]
                            if r0 == 0:
                                P.op("act", lambda e, dc=dc, b=b: e.copy(out=my[:, dc, :], in_=Bk(b)[:, 0:TT]), R=[bres(b)], W=["my"])
                            else:
                                P.op("dve", lambda e, dc=dc, b=b: e.tensor_tensor(out=my[:, dc, :], in0=my[:, dc, :], in1=Bk(b)[:, 0:TT],
                                                                                   op=ALU.add), R=[bres(b), "my"], W=["my"])
                def tail_out(src, srcres, ncol, dst_rows, stg_, sres, key):
                    for g0 in range(0, FC, 2):
                        n = min(2, FC - g0)
                        b = nbank()
                        for i in range(n):
                            P.op("pe", lambda e, b=b, i=i, g0=g0: e.transpose(Bk(b)[0:ncol, i * 128:(i + 1) * 128], src[:, g0 + i, :], ident),
                                 R=[srcres, "ident"], W=[bres(b)])
                        P.op("act", lambda e, b=b, n=n: e.copy(out=stg_[0:ncol, 0:n * 128], in_=Bk(b)[0:ncol, 0:n * 128]),
                             R=[bres(b)], W=[sres])
                        dma("sp", dst_rows[:, g0 * 128:(g0 + n) * 128], stg_[0:ncol, 0:n * 128], R=[sres], W=[], key=key)
                if has_s:
                    tail_out(gtail, "gtail", NS * 2, sco[l].rearrange("b j f -> (b j) f"), ostg, "ostg", "osc")
                if last_pass:
                    tail_out(ptail, "ptail", 2, pc[l], ostg2, "ostg" if has_s else "ostg2", "opc")
                dbg("myf_p%d_l%d" % (ps, l), my, "my", [128, KC, TT])
                fence()
                postnorm(l, "post_ffn")
                dbg("xout_p%d_l%d" % (ps, l), x, "x", [128, KC, TT])

            for l in range(L):
                do_layer(l)

            fence()
            stg = Bump(my_base, ARENA_WORDS)
            xo = stg([128, D], F32)
            for bi in range(NBLK):
                n = blk_rows(bi); c0, _ = blk_cols(bi)
                for g0 in range(0, KC, 4):
                    b = nbank()
                    for i in range(4):
                        kc = g0 + i
                        P.op("pe", lambda e, b=b, i=i, kc=kc, n=n, c0=c0: e.transpose(
                            Bk(b)[0:n, i * 128:(i + 1) * 128], x[:, kc, c0:c0 + n], ident), R=["x", "ident"], W=[bres(b)])
                    P.op("act", lambda e, b=b, g0=g0, n=n: e.copy(out=xo[0:n, g0 * 128:(g0 + 4) * 128], in_=Bk(b)[0:n, 0:512]),
                         R=[bres(b)], W=["xo"])
                dst = yp[tok0 + bi * 128: tok0 + bi * 128 + 128, :] if bi < NB else ys[:, :]
                dma("sp", dst, xo[0:n, :], R=["xo"], W=[], key="oy")
            return T

        tok0 = 0
        for ps in range(NPASS):
            tok0 += do_pass(ps, tok0)

        assert wstate["next_use"] == len(tiles), (wstate, len(tiles))

    import contextlib
    with contextlib.ExitStack() as es:
        arena = es.enter_context(nc.sbuf_tensor("arena", [128, ARENA_WORDS], F32))
        banks = [es.enter_context(nc.psum_tensor("bank%d" % i, [128, 512], F32)) for i in range(8)]
        body(arena, banks)
        sems = {e: es.enter_context(nc.semaphore("s_" + e)) for e in Prog.ENGS}
        dsems = {k: es.enter_context(nc.semaphore("d_" + k)) for k in P.dma_cnt}
        block = es.enter_context(nc.Block())
        P.emit(nc, block, sems, dsems)
    return nc


def make_consts(c):
    S, NS = c["S"], c["NS"]
    s = np.arange(128)[:, None]; t = np.arange(128)[None, :]
    mcur = (s <= t).astype(np.float32)
    mprev = (s > t).astype(np.float32)
    mhist = np.zeros((128, NS, S), np.float32)
    for b in range(NS):
        mhist[:, b, b * 8:(b + 1) * 8] = mprev[:, 0:8]
    mnew = np.zeros((S, S), np.float32)
    for b in range(NS):
        mnew[b * 8:(b + 1) * 8, b * 8:(b + 1) * 8] = mcur[0:8, 0:8]
    return dict(c_ident=np.eye(128, dtype=np.float32), c_mcur=mcur, c_mprev=mprev, c_mhist=mhist, c_mnew=mnew)


def run(cfg, inputs):
    c = derive(cfg)
    NC, NS, S, L = c["NCORES"], c["NS"], c["S"], c["DEPTH"]
    B = c["BATCH"]
    f = lambda a: np.ascontiguousarray(np.asarray(a, dtype=np.float32))
    consts = make_consts(c)
    shared = dict(w_in=f(inputs["w_in"]), w_out=f(inputs["w_out"]), w_gate=f(inputs["w_ffn_gate"]),
                  w_up=f(inputs["w_ffn_up"]), w_down=f(inputs["w_ffn_down"]), sinks=f(inputs["attn_sinks"]),
                  gms=f(inputs["gm_spatial"]), gmb=f(inputs["gm_bias"]), gmn=f(inputs["gm_norm"]),
                  g_pre_mix=f(inputs["norm_pre_mix"]), g_post_mix=f(inputs["norm_post_mix"]),
                  g_pre_ffn=f(inputs["norm_pre_ffn"]), g_post_ffn=f(inputs["norm_post_ffn"]),
                  cw=f(inputs["conv_w"]), cb=f(inputs["conv_b"]), **consts)
    x_prompt = f(inputs["x_prompt"]); x_sample = f(inputs["x_sample"])
    ssk = f(inputs["state_swa_k"]); ssv = f(inputs["state_swa_v"]); ssc = f(inputs["state_conv"])
    CSEQ, SEQ = c["CSEQ"], c["SEQ"]
    OWN_A = CSEQ
    split = CSEQ < SEQ
    b_start = SEQ - CSEQ
    in_maps = []
    for core in range(NC):
        sl = slice(core * NS, (core + 1) * NS)
        m = dict(shared)
        if split:
            isb = core % 2 == 1
            m["xp"] = np.ascontiguousarray(x_prompt[core // 2, (b_start if isb else 0):][:CSEQ])
            m["xprev"] = (np.ascontiguousarray(x_prompt[core // 2, b_start - 128:b_start]) if isb
                          else np.zeros((128, c["D"]), np.float32))
            m["c_flag"] = np.full((128, 2), 1.0 if isb else 0.0, np.float32)
        else:
            m["xp"] = x_prompt[core % B]
            m["xprev"] = np.zeros((128, c["D"]), np.float32)
            m["c_flag"] = np.zeros((128, 2), np.float32)
        m["xs"] = np.ascontiguousarray(x_sample[sl].reshape(S, c["D"]))
        m["sk"] = np.ascontiguousarray(ssk[:, sl].reshape(L, NS, 128, c["KVW"]))
        m["sv"] = np.ascontiguousarray(ssv[:, sl].reshape(L, NS, 128, c["KVW"]))
        m["sc"] = np.ascontiguousarray(ssc[:, sl])
        in_maps.append(m)
    nc = build_nc(cfg)
    res = run_bass_kernel_spmd(nc, in_maps, core_ids=list(range(NC)))
    R = res.results
    if DEBUG:
        LAST_DBG.clear()
        LAST_DBG.update({k: v for k, v in R[0].items() if k.startswith("dbg_")})
    D, F, KVW, GM = c["D"], c["F"], c["KVW"], c["GM"]
    if split:
        y_prompt = np.stack([np.concatenate([R[2 * b]["yp"][:OWN_A], R[2 * b + 1]["yp"][OWN_A - b_start:]], axis=0)
                             for b in range(B)]).astype(np.float32)
        last = [2 * b + 1 for b in range(B)]
    else:
        y_prompt = np.stack([R[b]["yp"] for b in range(B)]).astype(np.float32)
        last = list(range(B))
    y_sample = np.concatenate([R[k]["ys"].reshape(NS, 8, D) for k in range(NC)], axis=0)
    p_k = np.stack([R[b]["pk"] for b in last], axis=1).reshape(L, B, 128, 4, 64)
    p_v = np.stack([R[b]["pv"] for b in last], axis=1).reshape(L, B, 128, 4, 64)
    p_c = np.stack([R[b]["pc"] for b in last], axis=1)
    s_k = np.concatenate([R[k]["sko"].reshape(L, NS, 8, 4, 64) for k in range(NC)], axis=1)
    s_v = np.concatenate([R[k]["svo"].reshape(L, NS, 8, 4, 64) for k in range(NC)], axis=1)
    s_c = np.concatenate([R[k]["sco"] for k in range(NC)], axis=1)
    s_g = np.concatenate([R[k]["sgo"].reshape(L, NS, 8, c["GH"], 128) for k in range(NC)], axis=1)
    return (y_prompt, y_sample, p_k, p_v, p_c, s_k, s_v, s_c, s_g)


def kernel(**inputs):
    return run(FULL_CFG, inputs)
```

```python
import numpy as np
import concourse.bass as bass
import concourse.mybir as mybir
from concourse.bass_utils import run_bass_kernel_spmd

F32 = mybir.dt.float32
BF16 = mybir.dt.bfloat16
AF = mybir.ActivationFunctionType
ALU = mybir.AluOpType
AX = mybir.AxisListType

FULL_CFG = dict(D=4096, SEQ=2048, CSEQ=1152, BATCH=4, DEC_BATCH=32, DEC_SEQ=8, DEPTH=2,
                PASSES=[3, 3, 3], NCORES=8)
EPS = 1e-6
DEBUG = False
LAST_DBG = {}
ARENA_WORDS = 52000
WSLOTS = 3
WSLOT_ELEMS = 4096


def derive(cfg):
    c = dict(cfg)
    D = c["D"]
    c["ATT"] = D // 2
    c["GM"] = D - c["ATT"]
    c["HD"] = 64
    c["NH"] = c["ATT"] // 64
    c["NKV"] = 4
    c["GQA"] = c["NH"] // c["NKV"]
    c["CPK"] = c["GQA"] // 2
    c["GH"] = c["GM"] // 128
    c["F"] = ((8 * D // 3 + 255) // 256) * 256
    c["KC"] = D // 128
    c["FC"] = c["F"] // 128
    c["QC"] = c["ATT"] // 128
    c["UC"] = c["GM"] // 128
    c["KVW"] = c["NKV"] * 64
    c["IN"] = c["ATT"] + 2 * c["KVW"] + 2 * c["GM"]
    c["NS"] = c["DEC_BATCH"] // c["NCORES"]
    c["S"] = c["NS"] * c["DEC_SEQ"]
    c.setdefault("CSEQ", c["SEQ"])
    assert sum(c["PASSES"]) * 128 == c["CSEQ"]
    if c["CSEQ"] < c["SEQ"]:
        assert c["NCORES"] == 2 * c["BATCH"]
        assert c["SEQ"] - c["CSEQ"] >= 128 and c["SEQ"] - c["CSEQ"] + 2 * 128 <= c["CSEQ"]
    assert c["KC"] % 8 == 0 and c["CPK"] >= 1 and c["DEC_SEQ"] == 8 and c["NS"] == 4
    return c


class Prog:
    ENGS = ("pe", "act", "dve", "pool", "sp")

    def __init__(self):
        self.ops = {e: [] for e in self.ENGS}
        self.res = {}
        self.dma_cnt = {}

    def op(self, eng, fn, R=(), W=(), key=None):
        lst = self.ops[eng]
        me = (eng, len(lst))
        deps = set()
        for r in R:
            st = self.res.get(r)
            if st and st[0] is not None:
                deps.add(st[0])
        for w in W:
            st = self.res.get(w)
            if st:
                if st[0] is not None and st[0][0] != eng:
                    deps.add(st[0])
                for rd in st[1]:
                    if rd[0] != eng:
                        deps.add(rd)
        deps.discard(me)
        rec = dict(fn=fn, deps=deps, key=key, val=None, needed=False)
        if key is not None:
            self.dma_cnt[key] = self.dma_cnt.get(key, 0) + 1
            rec["val"] = 16 * self.dma_cnt[key]
        lst.append(rec)
        for d in deps:
            self.ops[d[0]][d[1]]["needed"] = True
        for r in R:
            self.res.setdefault(r, [None, []])[1].append(me)
        for w in W:
            self.res[w] = [me, []]
        return me

    def emit(self, nc, block, sems, dsems):
        cum = {}
        for e in self.ENGS:
            c = 0
            arr = []
            for rec in self.ops[e]:
                if rec["needed"] and rec["key"] is None:
                    c += 1
                arr.append(c)
            cum[e] = arr

        def target(dep):
            rec = self.ops[dep[0]][dep[1]]
            if rec["key"] is not None:
                return dsems[rec["key"]], rec["val"]
            return sems[dep[0]], cum[dep[0]][dep[1]]

        def run(e, handle):
            waited = {}
            for rec in self.ops[e]:
                need = {}
                for d in rec["deps"]:
                    s, v = target(d)
                    k = id(s)
                    if waited.get(k, (None, 0))[1] >= v:
                        continue
                    if k not in need or need[k][1] < v:
                        need[k] = (s, v)
                for k, (s, v) in need.items():
                    handle.wait_ge(s, v)
                    waited[k] = (s, v)
                ins = rec["fn"](handle)
                if rec["key"] is not None:
                    ins.then_inc(dsems[rec["key"]], 16)
                elif rec["needed"]:
                    ins.then_inc(sems[e], 1)
            done = {}
            for rec in self.ops[e]:
                if rec["key"] is not None:
                    done[rec["key"]] = max(done.get(rec["key"], 0), rec["val"])
            for k, v in done.items():
                handle.wait_ge(dsems[k], v)

        @block.tensor
        def _(t):
            run("pe", t)

        @block.scalar
        def _(a):
            run("act", a)

        @block.vector
        def _(v):
            run("dve", v)

        @block.gpsimd
        def _(g):
            run("pool", g)

        @block.sync
        def _(s):
            run("sp", s)


def build_nc(cfg):
    c = derive(cfg)
    D, KC, FC, F, QC, UC, GH = c["D"], c["KC"], c["FC"], c["F"], c["QC"], c["UC"], c["GH"]
    NKV, CPK, KVW, IN, GM, ATT = c["NKV"], c["CPK"], c["KVW"], c["IN"], c["GM"], c["ATT"]
    L, SEQ, S, NS = c["DEPTH"], c["CSEQ"], c["S"], c["NS"]
    PASSES = c["PASSES"]
    NPASS = len(PASSES)
    QSCALE = 64 ** -0.5
    RND = min(16, c["KC"])

    nc = bass.Bass("TRN2", target_bir_lowering=False)

    def din(name, shape):
        return nc.dram_tensor(name, list(shape), F32, kind="ExternalInput").ap()

    def dout(name, shape):
        return nc.dram_tensor(name, list(shape), F32, kind="ExternalOutput").ap()

    xp = din("xp", [SEQ, D]); xs = din("xs", [S, D])
    xprev = din("xprev", [128, D]); c_flag = din("c_flag", [128, 2])
    sk = din("sk", [L, NS, 128, KVW]); sv = din("sv", [L, NS, 128, KVW])
    sc = din("sc", [L, NS, 2, F])
    w_in = din("w_in", [L, D, IN]); w_out = din("w_out", [L, D, D])
    w_gate = din("w_gate", [L, D, F]); w_up = din("w_up", [L, D, F]); w_down = din("w_down", [L, F, D])
    sinks = din("sinks", [L, c["NH"]]); gms = din("gms", [L, GH, 128, 128]); gmb = din("gmb", [L, GH, 128])
    gmn = din("gmn", [L, GM])
    g_pre_mix = din("g_pre_mix", [L, D]); g_post_mix = din("g_post_mix", [L, D])
    g_pre_ffn = din("g_pre_ffn", [L, D]); g_post_ffn = din("g_post_ffn", [L, D])
    cw = din("cw", [L, 3, F]); cb = din("cb", [L, F])
    c_ident = din("c_ident", [128, 128]); c_mcur = din("c_mcur", [128, 128]); c_mprev = din("c_mprev", [128, 128])
    c_mhist = din("c_mhist", [128, NS, S]); c_mnew = din("c_mnew", [S, S])

    yp = dout("yp", [SEQ, D]); ys = dout("ys", [S, D])
    pk = dout("pk", [L, 128, KVW]); pv = dout("pv", [L, 128, KVW]); pc = dout("pc", [L, 2, F])
    sko = dout("sko", [L, S, KVW]); svo = dout("svo", [L, S, KVW]); sco = dout("sco", [L, NS, 2, F])
    sgo = dout("sgo", [L, S, GM])

    P = Prog()
    ctx = {}
    dbg_list = []

    def dbg(name, ap, res, shape):
        if not DEBUG or name in [d[0] for d in dbg_list]:
            return
        t = nc.dram_tensor("dbg_" + name, list(shape), F32, kind="ExternalOutput").ap()
        dbg_list.append((name, shape))
        eng = "pool" if ap.dtype == BF16 else "sp"
        P.op(eng, lambda e: e.dma_start(out=t, in_=ap), R=[res], W=[], key="dbg_" + name)

    def body(arena, banks):
        def carve(off, shape, dt):
            n = int(np.prod(shape[1:]))
            if dt == BF16:
                assert n % 2 == 0
                words = n // 2
                v = arena[:, off:off + words].bitcast(BF16)
            else:
                words = n
                v = arena[:, off:off + words]
            if len(shape) == 3:
                v = v.rearrange("p (a b) -> p a b", b=shape[2])
            elif len(shape) == 4:
                v = v.rearrange("p (a b c) -> p a b c", b=shape[2], c=shape[3])
            return v, off + words

        class Bump:
            def __init__(self, base, limit):
                self.o = base; self.limit = limit

            def __call__(self, shape, dt):
                v, self.o = carve(self.o, shape, dt)
                assert self.o <= self.limit, ("arena overflow", self.o, self.limit)
                return v

        pers = Bump(0, ARENA_WORDS)
        ident = pers([128, 128], F32)
        ones_b = pers([128, 128], BF16)
        mcur = pers([128, 128], BF16); mprev = pers([128, 128], BF16)
        mhist = pers([128, NS, S], BF16); mnew = pers([128, S], BF16)
        gains = {nm: pers([128, L * KC], F32) for nm in ("pre_mix", "post_mix", "pre_ffn", "post_ffn")}
        cwt = pers([128, L * 3 * FC], F32)
        cbt = pers([128, L * FC], F32)
        convcar = [pers([128, FC, 2], F32) for _ in range(L)]
        kcar = [pers([128, NKV, 128], BF16) for _ in range(L)]
        vcar = [pers([128, KVW], BF16) for _ in range(L)]
        sinkexp = pers([128, L * QC], F32)
        rstd = pers([128, 512], F32); rtmp = pers([128, 512], F32)
        tst = [pers([128, 128], F32) for _ in range(2)]
        epsb = pers([128, 2], F32)
        wslots = [pers([128, WSLOT_ELEMS], BF16) for _ in range(WSLOTS)]
        fcell = pers([128, 2], F32)
        flag = pers([128, 2], F32)
        mprev0 = pers([128, 128], BF16)
        PERS_END = pers.o

        OVL = ["my", "qu", "pre_x", "pre_h", "pre_sq", "bc", "xstage", "xo", "kdup", "kv32_0", "kv32_1", "gv32", "ss", "ggm_bc", "bbc", "wsT", "rt", "tmp",
               "skst", "skd", "wsamp32", "wsamp", "gext", "scs", "gtail", "ostg", "ptail", "ostg2"] + \
              ["vtm%d" % i for i in range(5)] + ["gv%d" % i for i in range(5)] + ["P%d" % i for i in range(6)] + \
              ["khd%d" % i for i in range(4)] + ["vh%d" % i for i in range(4)] + \
              ["cbuf%d" % i for i in range(4)] + ["gel%d" % i for i in range(4)]

        def fence():
            P.op("dve", lambda e: e.memset(fcell, 0.0), W=OVL)

        bank_rr = [0]

        def nbank():
            b = bank_rr[0] % 8
            bank_rr[0] += 1
            return b

        def Bk(i):
            return banks[i]

        def bres(i):
            return "bank%d" % i

        def dma(eng, out, in_, R, W, key, slow=False):
            if slow:
                P.op(eng, lambda e: e.dma_start(out=out, in_=in_, allow_slow_non_contiguous=True), R=R, W=W, key=key)
            else:
                P.op(eng, lambda e: e.dma_start(out=out, in_=in_), R=R, W=W, key=key)

        tcount = [0]

        def tload(dst, dstres, src_rows, R_):
            i = tcount[0] % 2
            tcount[0] += 1
            st = tst[i]
            dma("sp", st[0:R_, :], src_rows, R=[], W=["tst%d" % i], key="tst%d" % i)
            b = nbank()
            P.op("pe", lambda e: e.transpose(Bk(b)[:, 0:R_], st[0:R_, :], ident[0:R_, 0:R_]),
                 R=["tst%d" % i, "ident"], W=[bres(b)])
            P.op("act", lambda e: e.copy(out=dst, in_=Bk(b)[:, 0:R_]), R=[bres(b)], W=[dstres])

        dma("sp", ident, c_ident, R=[], W=["ident"], key="c0")
        dma("pool", mcur, c_mcur, R=[], W=["mcur"], key="c1")
        dma("pool", mprev, c_mprev, R=[], W=["mprev"], key="c2")
        dma("pool", mhist, c_mhist, R=[], W=["mhist"], key="c3")
        dma("pool", mnew[0:S, :], c_mnew, R=[], W=["mnew"], key="c4")
        P.op("dve", lambda e: e.memset(ones_b, 1.0), W=["ones"])
        P.op("dve", lambda e: e.memset(epsb, EPS), W=["eps"])
        for l in range(L):
            P.op("dve", lambda e, l=l: e.memset(convcar[l], 0.0), W=["convcar%d" % l])
            P.op("dve", lambda e, l=l: e.memset(kcar[l], 0.0), W=["kcar%d" % l])
            P.op("dve", lambda e, l=l: e.memset(vcar[l], 0.0), W=["vcar%d" % l])
        for nm, src in (("pre_mix", g_pre_mix), ("post_mix", g_post_mix), ("pre_ffn", g_pre_ffn), ("post_ffn", g_post_ffn)):
            rows = src.rearrange("l (k p) -> (l k) p", p=128)
            for r0 in range(0, L * KC, 128):
                n = min(128, L * KC - r0)
                tload(gains[nm][:, r0:r0 + n], "g_" + nm, rows[r0:r0 + n, :], n)
        rows = cw.rearrange("l j (f p) -> (l j f) p", p=128)
        for r0 in range(0, L * 3 * FC, 128):
            n = min(128, L * 3 * FC - r0)
            tload(cwt[:, r0:r0 + n], "cwt", rows[r0:r0 + n, :], n)
        rows = cb.rearrange("l (f p) -> (l f) p", p=128)
        for r0 in range(0, L * FC, 128):
            n = min(128, L * FC - r0)
            tload(cbt[:, r0:r0 + n], "cbt", rows[r0:r0 + n, :], n)
        for l in range(L):
            for par in range(2):
                src = sinks[l].rearrange("(c two) -> c two", two=2)[:, par].partition_broadcast(64)
                dma("sp", sinkexp[par * 64:(par + 1) * 64, l * QC:(l + 1) * QC], src, R=[], W=["sinkexp"],
                    key="c5", slow=True)
        P.op("act", lambda e: e.activation(out=sinkexp, in_=sinkexp, func=AF.Exp), R=["sinkexp"], W=["sinkexp"])

        def plan_tiles():
            tiles = []
            wv0 = w_in[0].rearrange("(k p) c -> p k c", p=128)
            for k0 in range(0, KC, 8):
                tiles.append((wv0[:, k0:k0 + 8, ATT:ATT + 2 * KVW], 8, 2 * KVW))
            for _p in range(NPASS):
                for l in range(L):
                    wv = w_in[l].rearrange("(k p) c -> p k c", p=128)
                    cols = [(c0, 512) for c0 in range(0, ATT, 512)] + [(ATT, 2 * KVW)] + \
                           [(c0, 512) for c0 in range(ATT + 2 * KVW, IN, 512)]
                    for (c0, w) in cols:
                        for k0 in range(0, KC, 8):
                            tiles.append((wv[:, k0:k0 + 8, c0:c0 + w], 8, w))
                    wv = w_out[l].rearrange("(k p) c -> p k c", p=128)
                    for c0 in range(0, D, 512):
                        for k0 in range(0, KC, 8):
                            tiles.append((wv[:, k0:k0 + 8, c0:c0 + 512], 8, 512))
                    gv_ = w_gate[l].rearrange("(k p) c -> p k c", p=128)
                    uv_ = w_up[l].rearrange("(k p) c -> p k c", p=128)
                    dv_ = w_down[l].rearrange("(f p) c -> p f c", p=128)
                    for r0 in range(0, FC, RND):
                        r1 = min(FC, r0 + RND)
                        for f0 in range(r0, r1, 4):
                            n = min(4, r1 - f0)
                            for wvv in (gv_, uv_):
                                for k0 in range(0, KC, 8):
                                    tiles.append((wvv[:, k0:k0 + 8, f0 * 128:(f0 + n) * 128], 8, n * 128))
                        for c0 in range(0, D, 512):
                            for t0 in range(r0, r1, 8):
                                n = min(8, r1 - t0)
                                tiles.append((dv_[:, t0:t0 + n, c0:c0 + 512], n, 512))
            return tiles

        tiles = plan_tiles()
        wstate = dict(next_issue=0, next_use=0)

        def w_issue():
            i = wstate["next_issue"]
            if i >= len(tiles):
                return
            src, a, b = tiles[i]
            s = i % WSLOTS
            dst = wslots[s][:, 0:a * b].rearrange("p (a b) -> p a b", b=b)
            dma("pool", dst, src, R=[], W=["wslot%d" % s], key="w%d" % s)
            wstate["next_issue"] += 1

        def w_next(a, b):
            i = wstate["next_use"]
            while wstate["next_issue"] < min(len(tiles), i + WSLOTS):
                w_issue()
            src, ta, tb = tiles[i]
            assert (ta, tb) == (a, b), (i, ta, tb, a, b)
            s = i % WSLOTS
            wstate["next_use"] += 1
            return wslots[s][:, 0:a * b].rearrange("p (a b) -> p a b", b=b), "wslot%d" % s


        def do_prepass():
            pb = Bump(PERS_END, ARENA_WORDS)
            xq = pb([128, KC, 128], F32)
            hq = pb([128, KC, 128], BF16)
            sqq = pb([128, KC, 128], BF16)
            xst = pb([128, D], F32)
            dma("sp", flag, c_flag, R=[], W=["flag"], key="c6")
            P.op("dve", lambda e: e.tensor_scalar(out=mprev0, in0=mprev, scalar1=flag[:, 0:1], scalar2=None, op0=ALU.mult),
                 R=["mprev", "flag"], W=["mprev0"])
            dma("sp", xst, xprev, R=[], W=["xstage"], key="xin")
            for g0 in range(0, KC, 4):
                b = nbank()
                for i in range(4):
                    kc = g0 + i
                    P.op("pe", lambda e, b=b, i=i, kc=kc: e.transpose(Bk(b)[:, i * 128:(i + 1) * 128], xst[:, kc * 128:(kc + 1) * 128], ident),
                         R=["xstage", "ident"], W=[bres(b)])
                P.op("act", lambda e, b=b, g0=g0: e.copy(out=xq[:, g0:g0 + 4, :], in_=Bk(b)[:, :].rearrange("p (a t) -> p a t", t=128)),
                     R=[bres(b)], W=["pre_x"])
            for g0 in range(0, KC, 8):
                P.op("act", lambda e, g0=g0: e.activation(out=sqq[:, g0:g0 + 8, :], in_=xq[:, g0:g0 + 8, :], func=AF.Square),
                     R=["pre_x"], W=["pre_sq"])
            b = nbank()
            for kc in range(KC):
                P.op("pe", lambda e, kc=kc, b=b: e.matmul(Bk(b)[:, 0:128], lhsT=ones_b, rhs=sqq[:, kc, :], start=(kc == 0), stop=(kc == KC - 1)),
                     R=["pre_sq", "ones"], W=[bres(b)])
            P.op("act", lambda e, b=b: e.activation(out=rtmp[:, 0:128], in_=Bk(b)[:, 0:128], func=AF.Sqrt, bias=epsb[:, 0:1], scale=1.0 / D),
                 R=[bres(b), "eps"], W=["rtmp"])
            P.op("dve", lambda e: e.reciprocal(out=rstd[:, 0:128], in_=rtmp[:, 0:128]), R=["rtmp"], W=["rstd"])
            g = gains["pre_mix"]
            for kc in range(KC):
                P.op("dve", lambda e, kc=kc: e.scalar_tensor_tensor(out=hq[:, kc, :], in0=xq[:, kc, :], scalar=g[:, kc:kc + 1], in1=rstd[:, 0:128],
                                                                    op0=ALU.mult, op1=ALU.mult), R=["pre_x", "rstd", "g_pre_mix"], W=["pre_h"])
            kb = [4, 5, 6, 7][:NKV]
            for k0 in range(0, KC, 8):
                wt, wres = w_next(8, 2 * KVW)
                for kl in range(8):
                    kc = k0 + kl
                    P.op("pe", lambda e, wt=wt, kl=kl, kc=kc: e.matmul(Bk(0)[:, 0:2 * KVW], lhsT=hq[:, kc, :], rhs=wt[:, kl, :],
                                                                      start=(kc == 0), stop=(kc == KC - 1)), R=[wres, "pre_h"], W=[bres(0)])
                    for j in range(NKV):
                        for par in range(2):
                            P.op("pe", lambda e, wt=wt, kl=kl, kc=kc, j=j, par=par: e.matmul(
                                Bk(kb[j])[par * 64:(par + 1) * 64, 0:128], lhsT=wt[:, kl, j * 64:(j + 1) * 64], rhs=hq[:, kc, :],
                                start=(kc == 0), stop=(kc == KC - 1)), R=[wres, "pre_h"], W=[bres(kb[j])])
            for j in range(NKV):
                P.op("act", lambda e, j=j: e.copy(out=kcar[0][:, j, :], in_=Bk(kb[j])[:, 0:128]), R=[bres(kb[j])], W=["kcar0"])
            P.op("act", lambda e: e.copy(out=vcar[0], in_=Bk(0)[:, KVW:2 * KVW]), R=[bres(0)], W=["vcar0"])

        do_prepass()

        def do_pass(ps, tok0):
            NB = PASSES[ps]
            T = NB * 128
            has_s = (ps == NPASS - 1)
            last_pass = (ps == NPASS - 1)
            SS = S if has_s else 0
            TT = T + SS
            assert TT <= 512
            NBLK = NB + (1 if has_s else 0)
            TTX = 2 + T + (NS * 10 if has_s else 0)

            pb = Bump(PERS_END, ARENA_WORDS)
            x = pb([128, KC, TT], F32)
            h = pb([128, KC, TT], BF16)
            qu = pb([128, KC, TT], BF16)
            QU_BASE = pb.o - (KC * TT) // 2
            my_base = pb.o
            my = pb([128, KC, TT], F32)
            MY_END = pb.o

            def blk_rows(bi):
                return 128 if bi < NB else SS

            def blk_cols(bi):
                return (bi * 128, 128) if bi < NB else (T, SS)

            fence()
            stg = Bump(my_base, ARENA_WORDS)
            xstage = stg([128, D], F32)
            for bi in range(NBLK):
                n = blk_rows(bi); c0, _ = blk_cols(bi)
                src = xp[tok0 + bi * 128: tok0 + bi * 128 + 128, :] if bi < NB else xs[:, :]
                dma("sp", xstage[0:n, :], src, R=[], W=["xstage"], key="xin")
                for g0 in range(0, KC, 4):
                    b = nbank()
                    for i in range(4):
                        kc = g0 + i
                        P.op("pe", lambda e, b=b, i=i, kc=kc, n=n: e.transpose(
                            Bk(b)[:, i * 128:i * 128 + n], xstage[0:n, kc * 128:(kc + 1) * 128], ident[0:n, 0:n]),
                            R=["xstage", "ident"], W=[bres(b)])
                    src_v = Bk(b)[:, :].rearrange("p (a t) -> p a t", t=128)[:, :, 0:n]
                    P.op("act", lambda e, g0=g0, c0=c0, n=n, src_v=src_v: e.copy(out=x[:, g0:g0 + 4, c0:c0 + n], in_=src_v),
                         R=[bres(b)], W=["x"])

            dbg("xin_p%d" % ps, x, "x", [128, KC, TT])

            def stat_rstd(src, srcres, sq, sqres, nfeat_chunks, outres):
                for g0 in range(0, nfeat_chunks, 4):
                    P.op("act", lambda e, g0=g0: e.activation(out=sq[:, g0:g0 + 4, :], in_=src[:, g0:g0 + 4, :], func=AF.Square),
                         R=[srcres], W=[sqres])
                b = nbank()
                for kc in range(nfeat_chunks):
                    P.op("pe", lambda e, kc=kc, b=b: e.matmul(Bk(b)[:, 0:TT], lhsT=ones_b, rhs=sq[:, kc, :],
                                                              start=(kc == 0), stop=(kc == nfeat_chunks - 1)),
                         R=[sqres, "ones"], W=[bres(b)])
                P.op("act", lambda e, b=b: e.activation(out=rtmp[:, 0:TT], in_=Bk(b)[:, 0:TT], func=AF.Sqrt,
                                                        bias=epsb[:, 0:1], scale=1.0 / (nfeat_chunks * 128)),
                     R=[bres(b), "eps"], W=["rtmp"])
                P.op("dve", lambda e: e.reciprocal(out=rstd[:, 0:TT], in_=rtmp[:, 0:TT]), R=["rtmp"], W=[outres])

            def prenorm(l, gname):
                stat_rstd(x, "x", qu, "qu", KC, "rstd")
                g = gains[gname]
                for kc in range(KC):
                    P.op("dve", lambda e, kc=kc: e.scalar_tensor_tensor(
                        out=h[:, kc, :], in0=x[:, kc, :], scalar=g[:, l * KC + kc:l * KC + kc + 1], in1=rstd[:, 0:TT],
                        op0=ALU.mult, op1=ALU.mult), R=["x", "rstd", "g_" + gname], W=["h"])

            def postnorm(l, gname):
                stat_rstd(my, "my", h, "h", KC, "rstd")
                g = gains[gname]
                for kc in range(KC):
                    P.op("dve", lambda e, kc=kc: e.scalar_tensor_tensor(
                        out=my[:, kc, :], in0=my[:, kc, :], scalar=g[:, l * KC + kc:l * KC + kc + 1], in1=rstd[:, 0:TT],
                        op0=ALU.mult, op1=ALU.mult), R=["my", "rstd", "g_" + gname], W=["my"])
                for g0 in range(0, KC, 4):
                    P.op("dve", lambda e, g0=g0: e.tensor_tensor(out=x[:, g0:g0 + 4, :], in0=x[:, g0:g0 + 4, :],
                                                                 in1=my[:, g0:g0 + 4, :], op=ALU.add),
                         R=["x", "my"], W=["x"])

            def fm_panel(hsrc, hres, nchunks, evac, bankset):
                w = nchunks * 128
                for k0 in range(0, KC, 8):
                    wt, wres = w_next(8, w)
                    for kl in range(8):
                        kc = k0 + kl
                        for i in range(nchunks):
                            b = bankset[i]
                            P.op("pe", lambda e, wt=wt, kl=kl, i=i, kc=kc, b=b: e.matmul(
                                Bk(b)[:, 0:TT], lhsT=wt[:, kl, i * 128:(i + 1) * 128], rhs=hsrc[:, kc, :],
                                start=(kc == 0), stop=(kc == KC - 1)), R=[wres, hres], W=[bres(b)])
                for i in range(nchunks):
                    evac(i, bankset[i])

            def do_layer(l):
                prenorm(l, "pre_mix")
                dbg("h_p%d_l%d" % (ps, l), h, "h", [128, KC, TT])
                dbg("rstd_p%d_l%d" % (ps, l), rstd[:, 0:TT], "rstd", [128, TT])
                fence()

                mt = Bump(my_base, ARENA_WORDS)
                kdup = mt([128, NKV, TT], BF16)
                vtm = [mt([128, KVW], BF16) for _ in range(NBLK)]
                kv32 = [mt([128, 2 * KVW], F32)] * 2
                gv = [mt([128, GM], BF16) for _ in range(NBLK)]
                ss = mt([128, 8 * NBLK], F32)
                ggm_bc = mt([128, GM], F32)
                bbc = ggm_bc.rearrange("p (h t) -> p h t", t=128)
                wsT = mt([128, GH, 128], BF16)
                Pb = [mt([128, 512], BF16) for _ in range(5)] + [None]
                rt = mt([128, 512], F32)
                tmp = mt([128, 512], F32)
                if has_s:
                    gv32 = mt([128, GM], F32)
                    khd = [mt([128, NKV, 128], BF16) for _ in range(NS)]
                    vh = [mt([128, KVW], BF16) for _ in range(NS)]
                    skst = rt[:, 0:KVW]
                    skd = tmp[:, 0:NKV * 128].rearrange("p (j a d) -> p j a d", a=2, d=64)
                    wsamp = mt([128, GH, S], BF16)

                dma("sp", ggm_bc, gmn[l].partition_broadcast(128), R=[], W=["bc"], key="bc0")
                for hd in range(GH):
                    i = tcount[0] % 2
                    tcount[0] += 1
                    st = tst[i]
                    dma("sp", st, gms[l, hd], R=[], W=["tst%d" % i], key="tst%d" % i)
                    b = nbank()
                    P.op("pe", lambda e, b=b, st=st: e.transpose(Bk(b)[:, 0:128], st, ident), R=["tst%d" % i, "ident"], W=[bres(b)])
                    P.op("dve", lambda e, b=b, hd=hd: e.tensor_tensor(out=wsT[:, hd, :], in0=Bk(b)[:, 0:128], in1=mcur, op=ALU.mult),
                         R=[bres(b), "mcur"], W=["wsT"])
                P.op("dve", lambda e: e.memset(ss, 0.0), W=["ss"])
                if has_s:
                    P.op("dve", lambda e: e.memset(wsamp, 0.0), W=["wsamp"])
                    for bq in range(NS):
                        dma("sp", wsamp[bq * 8:(bq + 1) * 8, :, bq * 8:(bq + 1) * 8], wsT[0:8, :, 0:8], R=["wsT"], W=["wsamp"],
                            key="wsamp")
                    for bq in range(NS):
                        dma("sp", skst, sk[l, bq], R=[], W=["rt"], key="skst")
                        P.op("dve", lambda e: e.tensor_copy(
                            out=skd, in_=skst.rearrange("p (j d) -> p j d", d=64).unsqueeze(2).to_broadcast([128, NKV, 2, 64])),
                            R=["rt"], W=["tmp"])
                        for j in range(NKV):
                            b = nbank()
                            P.op("pe", lambda e, b=b, j=j: e.transpose(Bk(b)[:, 0:128], skd[:, j].rearrange("p a d -> p (a d)"), ident),
                                 R=["tmp", "ident"], W=[bres(b)])
                            P.op("act", lambda e, b=b, j=j, bq=bq: e.copy(out=khd[bq][:, j, :], in_=Bk(b)[:, 0:128]),
                                 R=[bres(b)], W=["khd%d" % bq])
                        dma("pool", vh[bq], sv[l, bq], R=[], W=["vh%d" % bq], key="vh%d" % bq)

                pcount = [0]

                def bset():
                    s_ = [0, 1, 2, 3] if pcount[0] % 2 == 0 else [4, 5, 6, 7]
                    pcount[0] += 1
                    return s_

                for p0 in range(0, QC, 4):
                    def ev_q(i, b, p0=p0):
                        P.op("act", lambda e: e.mul(out=qu[:, p0 + i, :], in_=Bk(b)[:, 0:TT], mul=QSCALE),
                             R=[bres(b)], W=["qu"])
                    fm_panel(h, "h", 4, ev_q, bset())

                tb = list(range(NBLK))
                kb = [4, 5, 6, 7][:NKV]
                assert NBLK <= 4
                for k0 in range(0, KC, 8):
                    wt, wres = w_next(8, 2 * KVW)
                    for kl in range(8):
                        kc = k0 + kl
                        for bi in range(NBLK):
                            n = blk_rows(bi); c0, _ = blk_cols(bi)
                            P.op("pe", lambda e, wt=wt, kl=kl, kc=kc, bi=bi, n=n, c0=c0: e.matmul(
                                Bk(tb[bi])[0:n, 0:2 * KVW], lhsT=h[:, kc, c0:c0 + n], rhs=wt[:, kl, :],
                                start=(kc == 0), stop=(kc == KC - 1)), R=[wres, "h"], W=[bres(tb[bi])])
                        for j in range(NKV):
                            for par in range(2):
                                P.op("pe", lambda e, wt=wt, kl=kl, kc=kc, j=j, par=par: e.matmul(
                                    Bk(kb[j])[par * 64:(par + 1) * 64, 0:TT], lhsT=wt[:, kl, j * 64:(j + 1) * 64], rhs=h[:, kc, :],
                                    start=(kc == 0), stop=(kc == KC - 1)), R=[wres, "h"], W=[bres(kb[j])])
                for j in range(NKV):
                    P.op("act", lambda e, j=j: e.copy(out=kdup[:, j, :], in_=Bk(kb[j])[:, 0:TT]), R=[bres(kb[j])], W=["kdup"])
                for bi in range(NBLK):
                    n = blk_rows(bi)
                    P.op("act", lambda e, bi=bi, n=n: e.copy(out=vtm[bi][0:n, :], in_=Bk(tb[bi])[0:n, KVW:2 * KVW]),
                         R=[bres(tb[bi])], W=["vtm%d" % bi])
                    is_last_prompt = last_pass and bi == NB - 1
                    is_samp = bi >= NB
                    if is_last_prompt or is_samp:
                        kvs = kv32[bi % 2]
                        P.op("act", lambda e, bi=bi, n=n, kvs=kvs: e.copy(out=kvs[0:n, :], in_=Bk(tb[bi])[0:n, 0:2 * KVW]),
                             R=[bres(tb[bi])], W=["kv32_0"])
                        ko, vo = (sko[l], svo[l]) if is_samp else (pk[l], pv[l])
                        dma("sp", ko, kvs[0:n, 0:KVW], R=["kv32_0"], W=[], key="okv")
                        dma("sp", vo, kvs[0:n, KVW:2 * KVW], R=["kv32_0"], W=[], key="okv")

                dbg("kdup_p%d_l%d" % (ps, l), kdup, "kdup", [128, NKV, TT])
                dbg("q_p%d_l%d" % (ps, l), qu[:, 0:QC, :], "qu", [128, QC, TT])
                for p0 in range(0, UC, 4):
                    def ev_u(i, b, p0=p0):
                        P.op("act", lambda e: e.activation(out=qu[:, QC + p0 + i, :], in_=Bk(b)[:, 0:TT], func=AF.Gelu_apprx_tanh),
                             R=[bres(b)], W=["qu"])
                    fm_panel(h, "h", 4, ev_u, bset())

                NGP = GM // 512
                for gp in range(NGP):
                    bs_ = bset()
                    for k0 in range(0, KC, 8):
                        wt, wres = w_next(8, 512)
                        for kl in range(8):
                            kc = k0 + kl
                            for bi in range(NBLK):
                                n = blk_rows(bi); c0, _ = blk_cols(bi)
                                P.op("pe", lambda e, wt=wt, kl=kl, kc=kc, bi=bi, n=n, c0=c0, bs_=bs_: e.matmul(
                                    Bk(bs_[bi])[0:n, 0:512], lhsT=h[:, kc, c0:c0 + n], rhs=wt[:, kl, :],
                                    start=(kc == 0), stop=(kc == KC - 1)), R=[wres, "h"], W=[bres(bs_[bi])])
                    for bi in range(NBLK):
                        n = blk_rows(bi)
                        dst = gv32 if bi >= NB else gv[bi]
                        dres = "gv32" if bi >= NB else "gv%d" % bi
                        P.op("act", lambda e, bi=bi, n=n, dst=dst, gp=gp, bs_=bs_: e.activation(
                            out=dst[0:n, gp * 512:(gp + 1) * 512], in_=Bk(bs_[bi])[0:n, 0:512], func=AF.Gelu_apprx_tanh),
                            R=[bres(bs_[bi])], W=[dres])
                        P.op("act", lambda e, bi=bi, n=n, dst=dst, gp=gp: e.activation(
                            out=tmp[0:n, 0:512], in_=dst[0:n, gp * 512:(gp + 1) * 512], func=AF.Square,
                            accum_out=ss[0:n, bi * 8 + gp: bi * 8 + gp + 1]), R=[dres, "ss"], W=["tmp", "ss"])
                for bi in range(NBLK):
                    n = blk_rows(bi)
                    dst = gv32 if bi >= NB else gv[bi]
                    dres = "gv32" if bi >= NB else "gv%d" % bi
                    sc_ = ss[0:n, bi * 8 + 4: bi * 8 + 5]
                    sc2 = ss[0:n, bi * 8 + 5: bi * 8 + 6]
                    sc3 = ss[0:n, bi * 8 + 6: bi * 8 + 7]
                    P.op("dve", lambda e, bi=bi, n=n, sc_=sc_: e.reduce_sum(out=sc_, in_=ss[0:n, bi * 8: bi * 8 + NGP], axis=AX.X),
                         R=["ss"], W=["ss"])
                    P.op("act", lambda e, n=n, sc_=sc_, sc2=sc2: e.activation(out=sc2, in_=sc_, func=AF.Sqrt, bias=epsb[0:n, 0:1],
                                                                               scale=1.0 / GM), R=["ss", "eps"], W=["ss"])
                    P.op("dve", lambda e, sc2=sc2, sc3=sc3: e.reciprocal(out=sc3, in_=sc2), R=["ss"], W=["ss"])
                    P.op("dve", lambda e, n=n, dst=dst, sc3=sc3: e.scalar_tensor_tensor(
                        out=dst[0:n, :], in0=dst[0:n, :], scalar=sc3, in1=ggm_bc[0:n, :], op0=ALU.mult, op1=ALU.mult),
                        R=[dres, "ss", "bc"], W=[dres])
                    if bi >= NB:
                        dma("sp", sgo[l], gv32[0:n, :], R=["gv32"], W=[], key="osg")
                        P.op("act", lambda e, bi=bi, n=n: e.copy(out=gv[bi][0:n, :], in_=gv32[0:n, :]), R=["gv32"], W=["gv%d" % bi])

                dma("sp", ggm_bc, gmb[l].rearrange("h t -> (h t)").partition_broadcast(128), R=[], W=["bc"], key="bc0")
                pcnt = [0]
                scnt = [0]

                def attention_unit(c0, nt, sources, ures):
                    for j in range(NKV):
                        po = 4 + 2 * (pcnt[0] % 2)
                        pd = po + 1
                        pcnt[0] += 1
                        for par in range(2):
                            prs = slice(par * 64, (par + 1) * 64)
                            pbs = []
                            for si, (kd, kres, vv, vres, nk, mk, mres) in enumerate(sources):
                                sb = scnt[0] % 4
                                scnt[0] += 1
                                pidx = (par * 2 + si) % 5 if len(sources) <= 2 else si % 5
                                pbuf = Pb[pidx]
                                pres = "P%d" % pidx
                                P.op("pe", lambda e, sb=sb, kd=kd, nk=nk, prs=prs, j=j: e.matmul(
                                    Bk(sb)[0:nk, 0:CPK * nt].rearrange("p (c t) -> p c t", t=nt),
                                    lhsT=kd(j)[prs, 0:nk], rhs=qu[prs, j * CPK:(j + 1) * CPK, c0:c0 + nt],
                                    start=True, stop=True), R=[kres, "qu"], W=[bres(sb)])
                                P.op("act", lambda e, sb=sb, nk=nk, pbuf=pbuf: e.activation(
                                    out=pbuf[0:nk, 0:CPK * nt], in_=Bk(sb)[0:nk, 0:CPK * nt], func=AF.Exp),
                                    R=[bres(sb)], W=[pres])
                                P.op("dve", lambda e, nk=nk, pbuf=pbuf, mk=mk: e.tensor_tensor(
                                    out=pbuf[0:nk, 0:CPK * nt].rearrange("p (c t) -> p c t", t=nt),
                                    in0=pbuf[0:nk, 0:CPK * nt].rearrange("p (c t) -> p c t", t=nt),
                                    in1=mk.unsqueeze(1).to_broadcast([nk, CPK, nt]), op=ALU.mult),
                                    R=[pres, mres], W=[pres])
                                pbs.append((pbuf, pres, vv, vres, nk))
                            for si, (pbuf, pres, vv, vres, nk) in enumerate(pbs):
                                P.op("pe", lambda e, po=po, prs=prs, vv=vv, nk=nk, pbuf=pbuf, si=si, j=j: e.matmul(
                                    Bk(po)[prs, 0:CPK * nt], lhsT=vv[0:nk, j * 64:(j + 1) * 64], rhs=pbuf[0:nk, 0:CPK * nt],
                                    start=(si == 0), stop=(si == len(pbs) - 1)), R=[pres, vres], W=[bres(po)])
                            for si, (pbuf, pres, vv, vres, nk) in enumerate(pbs):
                                P.op("pe", lambda e, pd=pd, prs=prs, nk=nk, pbuf=pbuf, si=si: e.matmul(
                                    Bk(pd)[prs, 0:CPK * nt], lhsT=ones_b[0:nk, 0:64], rhs=pbuf[0:nk, 0:CPK * nt],
                                    start=(si == 0), stop=(si == len(pbs) - 1)), R=[pres, "ones"], W=[bres(pd)])
                        for i in range(CPK):
                            cidx = l * QC + j * CPK + i
                            P.op("dve", lambda e, pd=pd, i=i, cidx=cidx: e.tensor_scalar(
                                out=rt[:, i * nt:(i + 1) * nt], in0=Bk(pd)[:, i * nt:(i + 1) * nt],
                                scalar1=sinkexp[:, cidx:cidx + 1], scalar2=None, op0=ALU.add),
                                R=[bres(pd), "sinkexp"], W=["rt"])
                        P.op("dve", lambda e: e.reciprocal(out=rt[:, 0:CPK * nt], in_=rt[:, 0:CPK * nt]), R=["rt"], W=["rt"])
                        P.op("dve", lambda e, po=po, j=j: e.tensor_tensor(
                            out=qu[:, j * CPK:(j + 1) * CPK, c0:c0 + nt],
                            in0=Bk(po)[:, 0:CPK * nt].rearrange("p (c t) -> p c t", t=nt),
                            in1=rt[:, 0:CPK * nt].rearrange("p (c t) -> p c t", t=nt), op=ALU.mult),
                            R=[bres(po), "rt"], W=["qu"])

                for bi in range(NB):
                    srcs = []
                    first_of_seq = (ps == 0 and bi == 0 and l > 0)
                    if not first_of_seq:
                        if bi == 0:
                            mk0, mk0r = (mprev0, "mprev0") if ps == 0 else (mprev, "mprev")
                            srcs.append((lambda j: kcar[l][:, j, :], "kcar%d" % l, vcar[l], "vcar%d" % l, 128, mk0, mk0r))
                        else:
                            srcs.append((lambda j, bi=bi: kdup[:, j, (bi - 1) * 128: bi * 128], "kdup", vtm[bi - 1],
                                         "vtm%d" % (bi - 1), 128, mprev, "mprev"))
                    srcs.append((lambda j, bi=bi: kdup[:, j, bi * 128:(bi + 1) * 128], "kdup", vtm[bi], "vtm%d" % bi, 128, mcur, "mcur"))
                    attention_unit(bi * 128, 128, srcs, None)
                if has_s:
                    srcs = []
                    for bq in range(NS):
                        srcs.append((lambda j, bq=bq: khd[bq][:, j, :], "khd%d" % bq, vh[bq], "vh%d" % bq, 128, mhist[:, bq, :], "mhist"))
                    srcs.append((lambda j: kdup[:, j, T:T + S], "kdup", vtm[NB], "vtm%d" % NB, S, mnew[0:S, :], "mnew"))
                    attention_unit(T, S, srcs, None)
                if not last_pass:
                    P.op("act", lambda e: e.copy(out=kcar[l], in_=kdup[:, :, T - 128:T]), R=["kdup"], W=["kcar%d" % l])
                    P.op("act", lambda e: e.copy(out=vcar[l], in_=vtm[NB - 1]), R=["vtm%d" % (NB - 1)], W=["vcar%d" % l])

                for bi in range(NBLK):
                    n = blk_rows(bi); c0, nt = blk_cols(bi)
                    for hg in range(0, GH, 4):
                        b = nbank() % 4
                        for i in range(4):
                            hd = hg + i
                            rhs = wsT[:, hd, :] if bi < NB else wsamp[0:S, hd, :]
                            P.op("pe", lambda e, b=b, i=i, hd=hd, bi=bi, n=n, nt=nt, rhs=rhs: e.matmul(
                                Bk(b)[:, i * nt:(i + 1) * nt], lhsT=gv[bi][0:n, hd * 128:(hd + 1) * 128], rhs=rhs,
                                start=True, stop=True), R=["gv%d" % bi, "wsT", "wsamp"], W=[bres(b)])
                        if bi < NB:
                            bias_v = bbc[:, hg:hg + 4, :]
                            t3 = tmp[:, 0:4 * nt].rearrange("p (c t) -> p c t", t=nt)
                            b3 = Bk(b)[:, 0:4 * nt].rearrange("p (c t) -> p c t", t=nt)
                        else:
                            bias_v = bbc[:, hg:hg + 4, 0:8].unsqueeze(2).to_broadcast([128, 4, NS, 8])
                            t3 = tmp[:, 0:4 * nt].rearrange("p (c b t) -> p c b t", b=NS, t=8)
                            b3 = Bk(b)[:, 0:4 * nt].rearrange("p (c b t) -> p c b t", b=NS, t=8)
                        P.op("dve", lambda e, t3=t3, b3=b3, bias_v=bias_v: e.tensor_tensor(out=t3, in0=b3, in1=bias_v, op=ALU.add),
                             R=[bres(b), "bc"], W=["tmp"])
                        uq = qu[:, QC + hg:QC + hg + 4, c0:c0 + nt]
                        P.op("dve", lambda e, uq=uq, nt=nt: e.tensor_tensor(
                            out=uq, in0=uq, in1=tmp[:, 0:4 * nt].rearrange("p (c t) -> p c t", t=nt), op=ALU.mult),
                            R=["tmp", "qu"], W=["qu"])

                dbg("mix_p%d_l%d" % (ps, l), qu, "qu", [128, KC, TT])
                fence()
                for p0 in range(0, KC, 4):
                    def ev_m(i, b, p0=p0):
                        P.op("act", lambda e: e.copy(out=my[:, p0 + i, :], in_=Bk(b)[:, 0:TT]), R=[bres(b)],
                             W=["my"])
                    fm_panel(qu, "qu", 4, ev_m, bset())
                dbg("mym_p%d_l%d" % (ps, l), my, "my", [128, KC, TT])
                postnorm(l, "post_mix")
                dbg("xmid_p%d_l%d" % (ps, l), x, "x", [128, KC, TT])

                prenorm(l, "pre_ffn")
                fence()
                abuf, _ = carve(QU_BASE, [128, RND, TT], BF16)
                ft = Bump(MY_END, ARENA_WORDS)
                qt = Bump(QU_BASE + (RND * TT) // 2, my_base)
                if qt.o + 4 * TTX + 3 * TT > my_base:
                    qt = ft
                gext = qt([128, 4, TTX], F32)
                cbuf = [qt([128, TT], F32) for _ in range(2)] * 2
                gel = [qt([128, TT], F32)] * 4
                if has_s:
                    scs = ft([128, NS * 2 * FC], F32)
                    rows = sc[l].rearrange("b j (f p) -> (b j f) p", p=128)
                    for r0 in range(0, NS * 2 * FC, 128):
                        n = min(128, NS * 2 * FC - r0)
                        tload(scs[:, r0:r0 + n], "scs", rows[r0:r0 + n, :], n)
                    scs4 = scs.rearrange("p (b j f) -> p f b j", b=NS, j=2)
                    gtail = ft([128, FC, NS * 2], F32)
                    ostg = ft([128, 256], F32)
                if last_pass:
                    ptail = ft([128, FC, 2], F32)
                    ostg2 = ostg if has_s else ft([128, 256], F32)

                for r0 in range(0, FC, RND):
                    r1 = min(FC, r0 + RND)
                    for f0 in range(r0, r1, 4):
                        n = min(4, r1 - f0)
                        gb = [0, 1, 2, 3][:n]
                        ub = [4, 5, 6, 7][:n]
                        for (bset_, tag) in ((gb, "g"), (ub, "u")):
                            for k0 in range(0, KC, 8):
                                wt, wres = w_next(8, n * 128)
                                for kl in range(8):
                                    kc = k0 + kl
                                    for i in range(n):
                                        b = bset_[i]
                                        P.op("pe", lambda e, wt=wt, kl=kl, i=i, kc=kc, b=b: e.matmul(
                                            Bk(b)[:, 0:TT], lhsT=wt[:, kl, i * 128:(i + 1) * 128], rhs=h[:, kc, :],
                                            start=(kc == 0), stop=(kc == KC - 1)), R=[wres, "h"], W=[bres(b)])
                        P.op("dve", lambda e, f0=f0, n=n: e.tensor_copy(out=gext[:, 0:n, 0:2], in_=convcar[l][:, f0:f0 + n, :]),
                             R=["convcar%d" % l], W=["gext"])
                        if has_s:
                            ge_s = gext[:, :, 2 + T:2 + T + NS * 10].rearrange("p c (b t) -> p c b t", t=10)
                            P.op("dve", lambda e, f0=f0, n=n, ge_s=ge_s: e.tensor_copy(out=ge_s[:, 0:n, :, 0:2], in_=scs4[:, f0:f0 + n, :, :]),
                                 R=["scs"], W=["gext"])
                        for i in range(n):
                            P.op("act", lambda e, i=i, b=gb[i]: e.copy(out=gext[:, i, 2:2 + T], in_=Bk(b)[:, 0:T]), R=[bres(gb[i])], W=["gext"])
                            if has_s:
                                P.op("act", lambda e, i=i, ge_s=ge_s, b=gb[i]: e.copy(
                                    out=ge_s[:, i, :, 2:10], in_=Bk(b)[:, T:T + S].rearrange("p (b t) -> p b t", t=8)),
                                    R=[bres(gb[i])], W=["gext"])
                        if not last_pass:
                            P.op("dve", lambda e, f0=f0, n=n: e.tensor_copy(out=convcar[l][:, f0:f0 + n, :], in_=gext[:, 0:n, T:T + 2]),
                                 R=["gext"], W=["convcar%d" % l])
                        else:
                            P.op("dve", lambda e, f0=f0, n=n: e.tensor_copy(out=ptail[:, f0:f0 + n, :], in_=gext[:, 0:n, T:T + 2]),
                                 R=["gext"], W=["ptail"])
                        if has_s:
                            P.op("dve", lambda e, f0=f0, n=n, ge_s=ge_s: e.tensor_copy(
                                out=gtail[:, f0:f0 + n, :].rearrange("p f (b j) -> p f b j", j=2), in_=ge_s[:, 0:n, :, 8:10]),
                                R=["gext"], W=["gtail"])
                        for i0 in range(0, n, 2):
                          ii_ = list(range(i0, min(n, i0 + 2)))
                          for j3 in range(3):
                            for i in ii_:
                                fc = f0 + i
                                wcol = cwt[:, (l * 3 + j3) * FC + fc:(l * 3 + j3) * FC + fc + 1]
                                bcol = cbt[:, l * FC + fc:l * FC + fc + 1]
                                cbi = cbuf[i]
                                if j3 == 0:
                                    P.op("dve", lambda e, i=i, wcol=wcol, bcol=bcol, cbi=cbi: e.tensor_scalar(
                                        out=cbi[:, 0:T], in0=gext[:, i, 0:T], scalar1=wcol, scalar2=bcol, op0=ALU.mult, op1=ALU.add),
                                        R=["gext", "cwt", "cbt"], W=["cbuf%d" % (i % 2)])
                                    if has_s:
                                        P.op("dve", lambda e, i=i, wcol=wcol, bcol=bcol, cbi=cbi, ge_s=ge_s: e.tensor_scalar(
                                            out=cbi[:, T:T + S].rearrange("p (b t) -> p b t", t=8), in0=ge_s[:, i, :, 0:8],
                                            scalar1=wcol, scalar2=bcol, op0=ALU.mult, op1=ALU.add),
                                            R=["gext", "cwt", "cbt"], W=["cbuf%d" % (i % 2)])
                                else:
                                    P.op("dve", lambda e, i=i, wcol=wcol, cbi=cbi, j3=j3: e.scalar_tensor_tensor(
                                        out=cbi[:, 0:T], in0=gext[:, i, j3:j3 + T], scalar=wcol, in1=cbi[:, 0:T],
                                        op0=ALU.mult, op1=ALU.add), R=["gext", "cwt", "cbuf%d" % (i % 2)], W=["cbuf%d" % (i % 2)])
                                    if has_s:
                                        P.op("dve", lambda e, i=i, wcol=wcol, cbi=cbi, j3=j3, ge_s=ge_s: e.scalar_tensor_tensor(
                                            out=cbi[:, T:T + S].rearrange("p (b t) -> p b t", t=8), in0=ge_s[:, i, :, j3:j3 + 8],
                                            scalar=wcol, in1=cbi[:, T:T + S].rearrange("p (b t) -> p b t", t=8),
                                            op0=ALU.mult, op1=ALU.add), R=["gext", "cwt", "cbuf%d" % (i % 2)], W=["cbuf%d" % (i % 2)])
                          for i in ii_:
                            P.op("act", lambda e, i=i: e.activation(out=gel[i], in_=cbuf[i], func=AF.Gelu_apprx_tanh),
                                 R=["cbuf%d" % (i % 2)], W=["gel0"])
                            P.op("dve", lambda e, i=i, f0=f0, r0=r0, b=ub[i]: e.tensor_tensor(
                                out=abuf[:, f0 - r0 + i, :], in0=gel[i], in1=Bk(b)[:, 0:TT], op=ALU.mult),
                                R=["gel0", bres(ub[i])], W=["qu"])
                    nr = r1 - r0
                    for c0 in range(0, D, 512):
                        bs_ = bset()
                        for t0 in range(0, nr, 8):
                            n8 = min(8, nr - t0)
                            wt, wres = w_next(n8, 512)
                            for fl in range(n8):
                                fa = t0 + fl
                                for i in range(4):
                                    b = bs_[i]
                                    P.op("pe", lambda e, wt=wt, fl=fl, fa=fa, i=i, b=b, nr=nr: e.matmul(
                                        Bk(b)[:, 0:TT], lhsT=wt[:, fl, i * 128:(i + 1) * 128], rhs=abuf[:, fa, :],
                                        start=(fa == 0), stop=(fa == nr - 1)), R=[wres, "qu"], W=[bres(b)])
                        for i in range(4):
                            dc = c0 // 128 + i
                            b = bs_[i]
                            if r0 == 0:
                                P.op("act", lambda e, dc=dc, b=b: e.copy(out=my[:, dc, :], in_=Bk(b)[:, 0:TT]), R=[bres(b)], W=["my"])
                            else:
                                P.op("dve", lambda e, dc=dc, b=b: e.tensor_tensor(out=my[:, dc, :], in0=my[:, dc, :], in1=Bk(b)[:, 0:TT],
                                                                                   op=ALU.add), R=[bres(b), "my"], W=["my"])
                def tail_out(src, srcres, ncol, dst_rows, stg_, sres, key):
                    for g0 in range(0, FC, 2):
                        n = min(2, FC - g0)
                        b = nbank()
                        for i in range(n):
                            P.op("pe", lambda e, b=b, i=i, g0=g0: e.transpose(Bk(b)[0:ncol, i * 128:(i + 1) * 128], src[:, g0 + i, :], ident),
                                 R=[srcres, "ident"], W=[bres(b)])
                        P.op("act", lambda e, b=b, n=n: e.copy(out=stg_[0:ncol, 0:n * 128], in_=Bk(b)[0:ncol, 0:n * 128]),
                             R=[bres(b)], W=[sres])
                        dma("sp", dst_rows[:, g0 * 128:(g0 + n) * 128], stg_[0:ncol, 0:n * 128], R=[sres], W=[], key=key)
                if has_s:
                    tail_out(gtail, "gtail", NS * 2, sco[l].rearrange("b j f -> (b j) f"), ostg, "ostg", "osc")
                if last_pass:
                    tail_out(ptail, "ptail", 2, pc[l], ostg2, "ostg" if has_s else "ostg2", "opc")
                dbg("myf_p%d_l%d" % (ps, l), my, "my", [128, KC, TT])
                fence()
                postnorm(l, "post_ffn")
                dbg("xout_p%d_l%d" % (ps, l), x, "x", [128, KC, TT])

            for l in range(L):
                do_layer(l)

            fence()
            stg = Bump(my_base, ARENA_WORDS)
            xo = stg([128, D], F32)
            for bi in range(NBLK):
                n = blk_rows(bi); c0, _ = blk_cols(bi)
                for g0 in range(0, KC, 4):
                    b = nbank()
                    for i in range(4):
                        kc = g0 + i
                        P.op("pe", lambda e, b=b, i=i, kc=kc, n=n, c0=c0: e.transpose(
                            Bk(b)[0:n, i * 128:(i + 1) * 128], x[:, kc, c0:c0 + n], ident), R=["x", "ident"], W=[bres(b)])
                    P.op("act", lambda e, b=b, g0=g0, n=n: e.copy(out=xo[0:n, g0 * 128:(g0 + 4) * 128], in_=Bk(b)[0:n, 0:512]),
                         R=[bres(b)], W=["xo"])
                dst = yp[tok0 + bi * 128: tok0 + bi * 128 + 128, :] if bi < NB else ys[:, :]
                dma("sp", dst, xo[0:n, :], R=["xo"], W=[], key="oy")
            return T

        tok0 = 0
        for ps in range(NPASS):
            tok0 += do_pass(ps, tok0)

        assert wstate["next_use"] == len(tiles), (wstate, len(tiles))

    import contextlib
    with contextlib.ExitStack() as es:
        arena = es.enter_context(nc.sbuf_tensor("arena", [128, ARENA_WORDS], F32))
        banks = [es.enter_context(nc.psum_tensor("bank%d" % i, [128, 512], F32)) for i in range(8)]
        body(arena, banks)
        sems = {e: es.enter_context(nc.semaphore("s_" + e)) for e in Prog.ENGS}
        dsems = {k: es.enter_context(nc.semaphore("d_" + k)) for k in P.dma_cnt}
        block = es.enter_context(nc.Block())
        P.emit(nc, block, sems, dsems)
    return nc


def make_consts(c):
    S, NS = c["S"], c["NS"]
    s = np.arange(128)[:, None]; t = np.arange(128)[None, :]
    mcur = (s <= t).astype(np.float32)
    mprev = (s > t).astype(np.float32)
    mhist = np.zeros((128, NS, S), np.float32)
    for b in range(NS):
        mhist[:, b, b * 8:(b + 1) * 8] = mprev[:, 0:8]
    mnew = np.zeros((S, S), np.float32)
    for b in range(NS):
        mnew[b * 8:(b + 1) * 8, b * 8:(b + 1) * 8] = mcur[0:8, 0:8]
    return dict(c_ident=np.eye(128, dtype=np.float32), c_mcur=mcur, c_mprev=mprev, c_mhist=mhist, c_mnew=mnew)


def run(cfg, inputs):
    c = derive(cfg)
    NC, NS, S, L = c["NCORES"], c["NS"], c["S"], c["DEPTH"]
    B = c["BATCH"]
    f = lambda a: np.ascontiguousarray(np.asarray(a, dtype=np.float32))
    consts = make_consts(c)
    shared = dict(w_in=f(inputs["w_in"]), w_out=f(inputs["w_out"]), w_gate=f(inputs["w_ffn_gate"]),
                  w_up=f(inputs["w_ffn_up"]), w_down=f(inputs["w_ffn_down"]), sinks=f(inputs["attn_sinks"]),
                  gms=f(inputs["gm_spatial"]), gmb=f(inputs["gm_bias"]), gmn=f(inputs["gm_norm"]),
                  g_pre_mix=f(inputs["norm_pre_mix"]), g_post_mix=f(inputs["norm_post_mix"]),
                  g_pre_ffn=f(inputs["norm_pre_ffn"]), g_post_ffn=f(inputs["norm_post_ffn"]),
                  cw=f(inputs["conv_w"]), cb=f(inputs["conv_b"]), **consts)
    x_prompt = f(inputs["x_prompt"]); x_sample = f(inputs["x_sample"])
    ssk = f(inputs["state_swa_k"]); ssv = f(inputs["state_swa_v"]); ssc = f(inputs["state_conv"])
    CSEQ, SEQ = c["CSEQ"], c["SEQ"]
    OWN_A = CSEQ
    split = CSEQ < SEQ
    b_start = SEQ - CSEQ
    in_maps = []
    for core in range(NC):
        sl = slice(core * NS, (core + 1) * NS)
        m = dict(shared)
        if split:
            isb = core % 2 == 1
            m["xp"] = np.ascontiguousarray(x_prompt[core // 2, (b_start if isb else 0):][:CSEQ])
            m["xprev"] = (np.ascontiguousarray(x_prompt[core // 2, b_start - 128:b_start]) if isb
                          else np.zeros((128, c["D"]), np.float32))
            m["c_flag"] = np.full((128, 2), 1.0 if isb else 0.0, np.float32)
        else:
            m["xp"] = x_prompt[core % B]
            m["xprev"] = np.zeros((128, c["D"]), np.float32)
            m["c_flag"] = np.zeros((128, 2), np.float32)
        m["xs"] = np.ascontiguousarray(x_sample[sl].reshape(S, c["D"]))
        m["sk"] = np.ascontiguousarray(ssk[:, sl].reshape(L, NS, 128, c["KVW"]))
        m["sv"] = np.ascontiguousarray(ssv[:, sl].reshape(L, NS, 128, c["KVW"]))
        m["sc"] = np.ascontiguousarray(ssc[:, sl])
        in_maps.append(m)
    nc = build_nc(cfg)
    res = run_bass_kernel_spmd(nc, in_maps, core_ids=list(range(NC)))
    R = res.results
    if DEBUG:
        LAST_DBG.clear()
        LAST_DBG.update({k: v for k, v in R[0].items() if k.startswith("dbg_")})
    D, F, KVW, GM = c["D"], c["F"], c["KVW"], c["GM"]
    if split:
        y_prompt = np.stack([np.concatenate([R[2 * b]["yp"][:OWN_A], R[2 * b + 1]["yp"][OWN_A - b_start:]], axis=0)
                             for b in range(B)]).astype(np.float32)
        last = [2 * b + 1 for b in range(B)]
    else:
        y_prompt = np.stack([R[b]["yp"] for b in range(B)]).astype(np.float32)
        last = list(range(B))
    y_sample = np.concatenate([R[k]["ys"].reshape(NS, 8, D) for k in range(NC)], axis=0)
    p_k = np.stack([R[b]["pk"] for b in last], axis=1).reshape(L, B, 128, 4, 64)
    p_v = np.stack([R[b]["pv"] for b in last], axis=1).reshape(L, B, 128, 4, 64)
    p_c = np.stack([R[b]["pc"] for b in last], axis=1)
    s_k = np.concatenate([R[k]["sko"].reshape(L, NS, 8, 4, 64) for k in range(NC)], axis=1)
    s_v = np.concatenate([R[k]["svo"].reshape(L, NS, 8, 4, 64) for k in range(NC)], axis=1)
    s_c = np.concatenate([R[k]["sco"] for k in range(NC)], axis=1)
    s_g = np.concatenate([R[k]["sgo"].reshape(L, NS, 8, c["GH"], 128) for k in range(NC)], axis=1)
    return (y_prompt, y_sample, p_k, p_v, p_c, s_k, s_v, s_c, s_g)


def kernel(**inputs):
    return run(FULL_CFG, inputs)
```
